# Optimizing a Trainium2 kernel written in Bass

```python
import math
import jax, jax.numpy as jnp
from jax import lax
import numpy as np

D_MODEL = 1024
BATCH = 16
SEQ = 256
DEPTH = 2
DEC_BATCH = 8
DEC_SEQ = 2048
PAST_LEN = 256

GRID_W = 64
N_MIXERS = 2
N_DIR = 2
N_HEADS = 8
HEAD_K = 128
HEAD_V = 128
KEY_DIM = N_HEADS * HEAD_K
VAL_DIM = N_HEADS * HEAD_V
QKV_DIM = 2 * KEY_DIM + VAL_DIM
GDN_PROJ = QKV_DIM + VAL_DIM + 2 * N_DIR * N_HEADS
HGRN_PROJ = KEY_DIM + N_DIR * KEY_DIM + 2 * VAL_DIM
CONV_W = 5
GDN_CHUNK = 64
HGRN_CHUNK = 16
D_FF = 4 * D_MODEL
N_GDN = (DEPTH + 1) // 2
N_HGRN = DEPTH // 2
EPS = 1e-6
STATE_SCALE = 0.5

kernel_name = 'hybrid_gdn_hgrn2_flow_step'


def rms_norm(x, g):
    x32 = x.astype(jnp.float32)
    y = x32 * lax.rsqrt(jnp.mean(x32 * x32, axis=-1, keepdims=True) + EPS)
    return (y * g.astype(jnp.float32)).astype(x.dtype)


def l2_normalise(x):
    return x * lax.rsqrt(jnp.sum(x * x, axis=-1, keepdims=True) + EPS)


def centred_conv(x, w):
    pad = CONV_W // 2
    n = x.shape[-2]
    xp = jnp.pad(x, [(0, 0)] * (x.ndim - 2) + [(pad, pad), (0, 0)])
    return sum(xp[..., j:j + n, :] * w[j] for j in range(CONV_W))


def token_conv(x, w, grid):
    if not grid:
        return centred_conv(x, w)
    b, t, ch = x.shape
    rows = t // GRID_W
    return centred_conv(x.reshape(b, rows, GRID_W, ch), w).reshape(b, t, ch)


def to_heads(x, d):
    b, t, _ = x.shape
    return x.reshape(b, t, N_HEADS, d).transpose(0, 2, 1, 3)


def to_chunks(x, c):
    return x.reshape(x.shape[:2] + (x.shape[2] // c, c) + x.shape[3:])


def gated_delta_chunk(q, k, v, g, beta, s0):
    out_dtype = v.dtype
    q, k, v, g, beta, s = (a.astype(jnp.float32) for a in (q, k, v, g, beta, s0))
    b, h, t, dk = q.shape
    c = GDN_CHUNK
    q = to_chunks(q * dk ** -0.5, c)
    k = to_chunks(k, c)
    v = to_chunks(v, c)
    g = jnp.cumsum(to_chunks(g, c), axis=-1)
    beta = to_chunks(beta, c)
    causal = jnp.tril(jnp.ones((c, c), dtype=bool))
    strict = jnp.tril(jnp.ones((c, c), jnp.float32), -1)
    decay = jnp.exp(jnp.where(causal, g[..., :, None] - g[..., None, :], -jnp.inf))
    kb = k * beta[..., None]
    lower = jnp.einsum('bhncd,bhnsd->bhncs', kb, k) * decay * strict
    eye = jnp.eye(c, dtype=jnp.float32)
    tmat = lax.linalg.triangular_solve(eye + lower, jnp.broadcast_to(eye, lower.shape), left_side=True, lower=True)
    u = jnp.einsum('bhncs,bhnse->bhnce', tmat, v * beta[..., None])
    w = jnp.einsum('bhncs,bhnsd->bhncd', tmat, kb * jnp.exp(g)[..., None])
    qk = jnp.einsum('bhncd,bhnsd->bhncs', q, k) * decay
    q_dec = q * jnp.exp(g)[..., None]
    k_tail = k * jnp.exp(g[..., -1:] - g)[..., None]
    g_last = jnp.exp(g[..., -1])

    def step(state, xs):
        w_n, u_n, qd_n, qk_n, kt_n, gl_n = xs
        v_new = u_n - jnp.einsum('bhcd,bhde->bhce', w_n, state)
        o_n = jnp.einsum('bhcd,bhde->bhce', qd_n, state) + jnp.einsum('bhcs,bhse->bhce', qk_n, v_new)
        state = state * gl_n[..., None, None] + jnp.einsum('bhcd,bhce->bhde', kt_n, v_new)
        return state, o_n

    xs = tuple(jnp.moveaxis(a, 2, 0) for a in (w, u, q_dec, qk, k_tail, g_last))
    s, o = lax.scan(step, s, xs)
    o = jnp.moveaxis(o, 0, 2).reshape(b, h, t, v.shape[-1])
    return o.astype(out_dtype), s.astype(s0.dtype)


def hgrn2_chunk(q, k, v, logf, s0):
    out_dtype = v.dtype
    q, k, v, logf, s = (a.astype(jnp.float32) for a in (q, k, v, logf, s0))
    b, h, t, _ = q.shape
    c = HGRN_CHUNK
    q, k, v, logf = (to_chunks(a, c) for a in (q, k, v, logf))
    bcum = jnp.cumsum(logf, axis=-2)
    q_in = q * jnp.exp(bcum)
    k_out = k * jnp.exp(-bcum)
    causal = jnp.tril(jnp.ones((c, c), dtype=bool))
    att = jnp.where(causal, jnp.einsum('bhncd,bhnsd->bhncs', q_in, k_out), 0.0)
    o_intra = jnp.einsum('bhncs,bhnse->bhnce', att, v)
    b_last = bcum[..., -1:, :]
    k_tail = k * jnp.exp(b_last - bcum)
    f_last = jnp.exp(b_last[..., 0, :])

    def step(state, xs):
        qi_n, kt_n, v_n, fl_n = xs
        o_n = jnp.einsum('bhcd,bhde->bhce', qi_n, state)
        state = state * fl_n[..., :, None] + jnp.einsum('bhcd,bhce->bhde', kt_n, v_n)
        return state, o_n

    xs = tuple(jnp.moveaxis(a, 2, 0) for a in (q_in, k_tail, v, f_last))
    s, o_inter = lax.scan(step, s, xs)
    o = o_intra + jnp.moveaxis(o_inter, 0, 2)
    return o.reshape(b, h, t, v.shape[-1]).astype(out_dtype), s.astype(s0.dtype)


def flip_t(a):
    return jnp.flip(a, axis=2)


def gated_deltanet_mixer(h, w_in, conv_w, a_log, dt_bias, onorm_g, w_out, s0, grid):
    b, t, _ = h.shape
    proj = h @ w_in
    qkv, z, braw, araw = jnp.split(proj, [QKV_DIM, QKV_DIM + VAL_DIM, QKV_DIM + VAL_DIM + N_DIR * N_HEADS], axis=-1)
    qkv = jax.nn.silu(token_conv(qkv, conv_w, grid))
    q, k, v = jnp.split(qkv, [KEY_DIM, 2 * KEY_DIM], axis=-1)
    q = l2_normalise(to_heads(q, HEAD_K).astype(jnp.float32))
    k = l2_normalise(to_heads(k, HEAD_K).astype(jnp.float32))
    v = to_heads(v, HEAD_V)
    beta = jax.nn.sigmoid(braw.astype(jnp.float32)).reshape(b, t, N_DIR, N_HEADS)
    g = -jnp.exp(a_log.astype(jnp.float32)) * jax.nn.softplus(
        araw.astype(jnp.float32).reshape(b, t, N_DIR, N_HEADS) + dt_bias.astype(jnp.float32))
    outs, finals = [], []
    for d in range(N_DIR):
        args = (q, k, v, g[:, :, d].transpose(0, 2, 1), beta[:, :, d].transpose(0, 2, 1))
        if d:
            args = tuple(flip_t(a) for a in args)
        o_d, s_d = gated_delta_chunk(*args, s0[:, d])
        outs.append(flip_t(o_d) if d else o_d)
        finals.append(s_d)
    o = rms_norm(outs[0] + outs[1], onorm_g) * jax.nn.silu(to_heads(z, HEAD_V))
    o = o.transpose(0, 2, 1, 3).reshape(b, t, VAL_DIM) @ w_out
    return o, jnp.stack(finals, axis=1)


def hgrn2_mixer(h, w_in, lb, onorm_g, w_out, s0):
    b, t, _ = h.shape
    proj = h @ w_in
    q, f, i, z = jnp.split(proj, [KEY_DIM, 3 * KEY_DIM, 3 * KEY_DIM + VAL_DIM], axis=-1)
    q = to_heads(jax.nn.silu(q), HEAD_K)
    v = to_heads(i, HEAD_V)
    fgate = lb + (1.0 - lb) * jax.nn.sigmoid(f.astype(jnp.float32).reshape(b, t, N_DIR, KEY_DIM))
    outs, finals = [], []
    for d in range(N_DIR):
        fd = to_heads(fgate[:, :, d], HEAD_K)
        args = (q, 1.0 - fd, v, jnp.log(fd))
        if d:
            args = tuple(flip_t(a) for a in args)
        o_d, s_d = hgrn2_chunk(*args, s0[:, d])
        outs.append(flip_t(o_d) if d else o_d)
        finals.append(s_d)
    o = rms_norm(outs[0] + outs[1], onorm_g) * jax.nn.silu(to_heads(z, HEAD_V))
    o = o.transpose(0, 2, 1, 3).reshape(b, t, VAL_DIM) @ w_out
    return o, jnp.stack(finals, axis=1)


def layer_lower_bounds(lb_logits):
    p = jax.nn.softmax(lb_logits.astype(jnp.float32), axis=0)
    return jnp.cumsum(p, axis=0) - p[0]


def run_trunk(x, cond, s_gdn, s_hgrn, grid, weights):
    (w_ada, b_ada, norm_g, gdn_w_in, gdn_conv_w, gdn_a_log, gdn_dt_bias, gdn_onorm_g, gdn_w_out,
     hgrn_w_in, hgrn_lb_logits, hgrn_onorm_g, hgrn_w_out, mlp_w1, mlp_w2) = weights
    lb_all = layer_lower_bounds(hgrn_lb_logits)
    cond_act = jax.nn.silu(cond)
    new_gdn, new_hgrn = [], []
    for layer in range(DEPTH):
        mod = (cond_act @ w_ada[layer] + b_ada[layer])[:, None, :]
        sh1, sc1, gt1, sh2, sc2, gt2 = jnp.split(mod, 6, axis=-1)
        hmod = rms_norm(x, norm_g[layer, 0]) * (1.0 + sc1) + sh1
        j = layer // N_MIXERS
        if layer % N_MIXERS == 0:
            mix, s_fin = gated_deltanet_mixer(hmod, gdn_w_in[j], gdn_conv_w[j], gdn_a_log[j], gdn_dt_bias[j],
                                              gdn_onorm_g[j], gdn_w_out[j], s_gdn[:, j], grid)
            new_gdn.append(s_fin)
        else:
            mix, s_fin = hgrn2_mixer(hmod, hgrn_w_in[j], lb_all[layer], hgrn_onorm_g[j], hgrn_w_out[j], s_hgrn[:, j])
            new_hgrn.append(s_fin)
        x = x + gt1 * rms_norm(mix, norm_g[layer, 1])
        hmod = rms_norm(x, norm_g[layer, 2]) * (1.0 + sc2) + sh2
        ff = jnp.square(jax.nn.relu(hmod @ mlp_w1[layer])) @ mlp_w2[layer]
        x = x + gt2 * rms_norm(ff, norm_g[layer, 3])
    return x, jnp.stack(new_gdn, axis=1), jnp.stack(new_hgrn, axis=1)


def setup_inputs(seed: int = 0) -> dict:
    key = jax.random.key(seed)
    ks = jax.random.split(key, 22)
    f32 = jnp.float32

    def nrm(k, shape, scale):
        return jax.random.normal(k, shape, f32) * scale

    dt = jnp.exp(jax.random.uniform(ks[12], (N_GDN, N_DIR, N_HEADS), f32, math.log(1e-3), math.log(1e-1)))
    return {
        'x_prompt': nrm(ks[0], (BATCH, SEQ, D_MODEL), 1.0),
        'x_sample': nrm(ks[1], (DEC_BATCH, DEC_SEQ, D_MODEL), 1.0),
        'state_gdn': nrm(ks[2], (DEC_BATCH, N_GDN, N_DIR, N_HEADS, HEAD_K, HEAD_V), STATE_SCALE),
        'state_hgrn': nrm(ks[3], (DEC_BATCH, N_HGRN, N_DIR, N_HEADS, HEAD_K, HEAD_V), STATE_SCALE),
        'c': nrm(ks[4], (DEC_BATCH, D_MODEL), 1.0),
        'c_ctx': nrm(ks[5], (D_MODEL,), 1.0),
        'w_ada': nrm(ks[6], (DEPTH, D_MODEL, 6 * D_MODEL), 0.5 * D_MODEL ** -0.5),
        'b_ada': nrm(ks[7], (DEPTH, 6 * D_MODEL), 0.02),
        'norm_g': 1.0 + nrm(ks[8], (DEPTH, 4, D_MODEL), 0.05),
        'gdn_w_in': nrm(ks[9], (N_GDN, D_MODEL, GDN_PROJ), D_MODEL ** -0.5),
        'gdn_conv_w': nrm(ks[10], (N_GDN, CONV_W, QKV_DIM), CONV_W ** -0.5),
        'gdn_a_log': jnp.log(jax.random.uniform(ks[11], (N_GDN, N_DIR, N_HEADS), f32, 1.0, 16.0)),
        'gdn_dt_bias': dt + jnp.log(-jnp.expm1(-dt)),
        'gdn_onorm_g': 1.0 + nrm(ks[13], (N_GDN, HEAD_V), 0.05),
        'gdn_w_out': nrm(ks[14], (N_GDN, VAL_DIM, D_MODEL), VAL_DIM ** -0.5),
        'hgrn_w_in': nrm(ks[15], (N_HGRN, D_MODEL, HGRN_PROJ), D_MODEL ** -0.5),
        'hgrn_lb_logits': nrm(ks[16], (DEPTH, N_DIR, KEY_DIM), 0.5),
        'hgrn_onorm_g': 1.0 + nrm(ks[17], (N_HGRN, HEAD_V), 0.05),
        'hgrn_w_out': nrm(ks[18], (N_HGRN, VAL_DIM, D_MODEL), VAL_DIM ** -0.5),
        'mlp_w1': nrm(ks[19], (DEPTH, D_MODEL, D_FF), D_MODEL ** -0.5),
        'mlp_w2': nrm(ks[20], (DEPTH, D_FF, D_MODEL), D_FF ** -0.5),
    }


def reference(x_prompt, x_sample, state_gdn, state_hgrn, c, c_ctx, w_ada, b_ada, norm_g,
              gdn_w_in, gdn_conv_w, gdn_a_log, gdn_dt_bias, gdn_onorm_g, gdn_w_out,
              hgrn_w_in, hgrn_lb_logits, hgrn_onorm_g, hgrn_w_out, mlp_w1, mlp_w2):
    weights = (w_ada, b_ada, norm_g, gdn_w_in, gdn_conv_w, gdn_a_log, gdn_dt_bias, gdn_onorm_g, gdn_w_out,
               hgrn_w_in, hgrn_lb_logits, hgrn_onorm_g, hgrn_w_out, mlp_w1, mlp_w2)
    nb = x_prompt.shape[0]
    zeros_gdn = jnp.zeros((nb, N_GDN, N_DIR, N_HEADS, HEAD_K, HEAD_V), x_prompt.dtype)
    zeros_hgrn = jnp.zeros((nb, N_HGRN, N_DIR, N_HEADS, HEAD_K, HEAD_V), x_prompt.dtype)
    y_prompt, new_state_gdn, new_state_hgrn = run_trunk(x_prompt, c_ctx[None, :], zeros_gdn, zeros_hgrn, False, weights)
    y_sample, _, _ = run_trunk(x_sample, c, state_gdn, state_hgrn, True, weights)
    return (y_prompt, y_sample, new_state_gdn, new_state_hgrn)
```

```python
import contextlib
import os
import numpy as np
import concourse.bass as bass
import concourse.mybir as mybir
from concourse.bass_utils import run_bass_kernel_spmd

F32, BF16 = mybir.dt.float32, mybir.dt.bfloat16
AF = mybir.ActivationFunctionType
ALU = mybir.AluOpType

T = 2560
OPSB = int(os.environ.get('OPSB', '4'))
BARR = int(os.environ.get('BARR', '0'))
NCH = 5
NTL = 20
SEQS = [(0, 16), (16, 2), (18, 2)]
EPS = 1e-6
D = 1024

C_ONES, C_ID, C_MF64, C_MB64, C_BLK64, C_SF64, C_SB64, C_MF16, C_MB16, C_SF16, C_SB16, C_CM = [i * 128 for i in range(12)]
NCONST = 12 * 128


_UN = [0]


def _un(name):
    _UN[0] += 1
    return f"sb{_UN[0]}_{name}"


def make_consts():
    t = np.arange(128)
    blk64 = (t[:, None] // 64) == (t[None, :] // 64)
    blk16 = (t[:, None] // 16) == (t[None, :] // 16)
    le = t[:, None] <= t[None, :]
    ge = t[:, None] >= t[None, :]
    lt = t[:, None] < t[None, :]
    gt = t[:, None] > t[None, :]
    cm = np.zeros((128, 128), np.float32)
    cm[t, t // 16] = 1.0
    mats = [np.ones((128, 128)), np.eye(128), blk64 & le, blk64 & ge, blk64, blk64 & lt, blk64 & gt,
            blk16 & le, blk16 & ge, blk16 & lt, blk16 & gt, cm]
    return np.ascontiguousarray(np.concatenate([m.astype(np.float32) for m in mats], axis=1))


class Sched:
    EPOCH = 1 << 30

    def __init__(self, nc, stack):
        self.nc = nc
        self.stack = stack
        self.eng = {'pe': nc.tensor, 'act': nc.scalar, 'dve': nc.vector, 'pool': nc.gpsimd, 'sp': nc.sync}
        self.cur = {}
        self.waited = {}
        self.lw = {}
        self.rd = {}
        self.nsem = 0
        self.dsem = {}
        self.allsems = {}

    def newsem(self):
        self.nsem += 1
        s = self.stack.enter_context(self.nc.semaphore(f"s{self.nsem}"))
        self.allsems[id(s)] = s
        return s

    def _tick(self, e):
        c = self.cur.get(e)
        if c is None or c[1] >= self.EPOCH:
            c = [self.newsem(), 0]
            self.cur[e] = c
        c[1] += 1
        return (c[0], c[1], e)

    def _wait(self, e, tok):
        sem, val, _ = tok
        k = (e, id(sem))
        if self.waited.get(k, 0) >= val:
            return
        self.waited[k] = val
        self.eng[e].wait_ge(sem, val)

    def deps(self, e, r, w):
        toks = []
        for k in r:
            t = self.lw.get(k)
            if t:
                toks.append(t)
        for k in w:
            t = self.lw.get(k)
            if t:
                toks.append(t)
            toks.extend(self.rd.get(k, {}).values())
        for t in toks:
            if e == 'pe' and t[2] == 'pe':
                continue
            self._wait(e, t)

    def commit(self, tok, r, w):
        for k in r:
            d = self.rd.setdefault(k, {})
            o = d.get(id(tok[0]))
            if o is None or o[1] < tok[1]:
                d[id(tok[0])] = tok
        for k in w:
            self.lw[k] = tok
            self.rd[k] = {}

    def op(self, e, fn, r=(), w=()):
        w = list(w) + [k for k in r if isinstance(k, tuple) and k and k[0] == 'ps' and k not in w]
        self.deps(e, r, w)
        inst = fn()
        tok = self._tick(e)
        inst.then_inc(tok[0], 1)
        self.commit(tok, r, w)

    def dma(self, e, pairs, r=(), w=(), slot=None):
        self.deps(e, r, w)
        if isinstance(slot, tuple) and slot[0] == 'dbg':
            slot = 'dbg'
        ds = self.dsem.get(slot)
        if ds is None:
            ds = [self.newsem(), 0]
            self.dsem[slot] = ds
        for out, in_ in pairs:
            inst = self.eng[e].dma_start(out=out, in_=in_)
            ds[1] += 16
            inst.then_inc(ds[0], 16)
        tok = (ds[0], ds[1], 'dma')
        self.commit(tok, r, w)

    def barrier(self):
        toks = []
        for e, c in self.cur.items():
            toks.append((c[0], c[1], e))
        for s, ds in self.dsem.items():
            toks.append((ds[0], ds[1], 'dma'))
        for e in ('pe', 'act', 'dve', 'pool', 'sp'):
            for t in toks:
                if t[1] > 0:
                    self._wait(e, t)

    def final(self):
        for s, ds in self.dsem.items():
            if ds[1] > 0:
                self._wait('sp', (ds[0], ds[1], 'dma'))
        for e, c in self.cur.items():
            if e != 'sp':
                self._wait('sp', (c[0], c[1], e))


def build(stop_after=None, dbg=None):
    nc = bass.Bass("TRN2", target_bir_lowering=False)
    dbg = dbg or []

    def din(name, shape):
        return nc.dram_tensor(name, list(shape), F32, kind="ExternalInput").ap()

    def dout(name, shape):
        return nc.dram_tensor(name, list(shape), F32, kind="ExternalOutput").ap()

    xT = din("xT", [D, T])
    cT = din("cT", [D, 2])
    consts_d = din("consts", [128, NCONST])
    w_ada = din("w_ada", [2, D, 6 * D])
    b_adaT = din("b_adaT", [128, 2, 48])
    norm_gT = din("norm_gT", [128, 2, 4, 8])
    gdn_w_in = din("gdn_w_in", [D, 4128])
    conv_wT = din("conv_wT", [128, 24, 5])
    alog10 = din("alog10", [128, 160])
    dtb10 = din("dtb10", [128, 160])
    gdn_on = din("gdn_on", [128, 1])
    gdn_w_out = din("gdn_w_out", [D, D])
    hgrn_w_in = din("hgrn_w_in", [D, 5120])
    lblT = din("lblT", [128, 2, 16])
    hgrn_on = din("hgrn_on", [128, 1])
    hgrn_w_out = din("hgrn_w_out", [D, D])
    mlp_w1 = din("mlp_w1", [2, D, 4 * D])
    mlp_w2 = din("mlp_w2", [2, 4 * D, D])
    sg0 = din("sg0", [2, 8, 128, 128])
    sh0 = din("sh0", [2, 8, 128, 128])
    yT = dout("yT", [D, T])
    sgo = dout("sgo", [2, 2, 8, 128, 128])
    sho = dout("sho", [2, 2, 8, 128, 128])
    xs = nc.dram_tensor("xs", [D, T], F32, kind="Internal").ap()
    dbg_out = {}
    for name, shape, dt in dbg:
        dbg_out[name] = nc.dram_tensor("dbg_" + name, list(shape), dt, kind="ExternalOutput").ap()

    xT_v = xT.rearrange("(k p) t -> p k t", p=128)
    xs_v = xs.rearrange("(k p) t -> p k t", p=128)
    yT_v = yT.rearrange("(k p) t -> p k t", p=128)

    class Stop(Exception):
        pass

    with contextlib.ExitStack() as stack:
        sch = Sched(nc, stack)
        E = stack.enter_context

        def sb(name, shape, dt):
            return E(nc.sbuf_tensor(_un(name), list(shape), dt))

        consts = sb("consts", [128, NCONST], F32)
        onesb = sb("onesb", [128, 128], BF16)
        identb = sb("identb", [128, 128], BF16)
        cin = sb("cin", [128, 8, 2], F32)
        scT = sb("scT", [128, 8, 2], F32)
        badat = sb("badat", [128, 2, 48], F32)
        normg = sb("normg", [128, 2, 4, 8], F32)
        mod = sb("mod", [128, 2, 48, 2], F32)
        der = sb("der", [128, 2, 4, 8, 2], F32)
        hA = sb("hA", [128, 8, T], BF16)
        slabs = [sb(f"slab{i}", [128, 8192], BF16) for i in range(2)]
        psb_ = [E(nc.psum_tensor(f"ps{i}", [128, 512], F32)) for i in range(8)]
        pst = None

        def cst(off, n=128):
            return consts[:, off:off + n]

        st = {'pb': 0, 'slab': 0}

        def psbank():
            b = st['pb'] % int(os.environ.get('NRING', '4'))
            st['pb'] += 1
            return psb_[b], ('ps', b)

        def psq():
            t_, k_ = psbank()
            return t_[:, 0:128], k_

        def psh():
            t_, k_ = psbank()
            return t_[:, 0:256], [k_]

        def next_slab():
            i = st['slab'] % 2
            st['slab'] += 1
            return slabs[i], ('slab', i)

        def mm_group(out, pairs):
            def fn():
                n = len(pairs)
                inst = None
                for i, (l, r_) in enumerate(pairs):
                    inst = nc.tensor.matmul(out, lhsT=l, rhs=r_, start=(i == 0), stop=(i == n - 1))
                return inst
            return fn

        def dump(name, ap, key):
            if name in dbg_out:
                sch.dma('sp', [(dbg_out[name], ap)], r=[key], w=[('dbgout', name)], slot=('dbg', name))

        def program():
            sch.dma('sp', [(consts[:], consts_d), (cin[:], cT.rearrange("(k p) c -> p k c", p=128)), (badat[:], b_adaT), (normg[:], norm_gT)],
                    w=['consts', 'cin', 'badat', 'normg'], slot='setup')
            sch.op('dve', lambda: nc.vector.tensor_copy(out=onesb[:], in_=cst(C_ONES)), r=['consts'], w=['onesb'])
            sch.op('dve', lambda: nc.vector.tensor_copy(out=identb[:], in_=cst(C_ID)), r=['consts'], w=['identb'])
            sch.op('act', lambda: nc.scalar.activation(out=scT[:], in_=cin[:], func=AF.Silu), r=['cin'], w=['scT'])

            es_a = contextlib.ExitStack()
            aslab = [es_a.enter_context(nc.sbuf_tensor(_un(f"aslab{i}"), [128, 8192], F32)) for i in range(2)]
            na = 0
            for l in range(2):
                mps, mkey = psbank()
                for s in range(6):
                    ai = na % 2
                    na += 1
                    akey = ('aslab', ai)
                    sv = aslab[ai][:, 0:8192].rearrange("p (k n) -> p k n", k=8)
                    sch.dma('sp', [(sv, w_ada[l, :, s * 1024:(s + 1) * 1024].rearrange("(k p) n -> p k n", p=128))],
                            w=[akey], slot=('xin', ai))
                    for j in range(8):
                        col = (s * 8 + j) * 2
                        sch.op('pe', mm_group(mps[:, col:col + 2],
                                              [(sv[:, k, j * 128:(j + 1) * 128], scT[:, k, :]) for k in range(8)]),
                               r=[akey, 'scT'], w=[mkey])
                mv = mps[:, 0:96].rearrange("p (j c) -> p j c", c=2)
                for c in range(2):
                    sch.op('dve', lambda c=c: nc.vector.tensor_tensor(out=mod[:, l, :, c], in0=mv[:, :, c], in1=badat[:, l, :], op=ALU.add),
                           r=[mkey, 'badat'], w=['mod'])
                for c in range(2):
                    sch.op('dve', lambda c=c: nc.vector.scalar_tensor_tensor(out=der[:, l, 0, :, c], in0=mod[:, l, 8:16, c], scalar=1.0, in1=normg[:, l, 0, :], op0=ALU.add, op1=ALU.mult), r=['mod', 'normg'], w=['der'])
                    sch.op('dve', lambda c=c: nc.vector.tensor_tensor(out=der[:, l, 1, :, c], in0=mod[:, l, 16:24, c], in1=normg[:, l, 1, :], op=ALU.mult), r=['mod', 'normg'], w=['der'])
                    sch.op('dve', lambda c=c: nc.vector.scalar_tensor_tensor(out=der[:, l, 2, :, c], in0=mod[:, l, 32:40, c], scalar=1.0, in1=normg[:, l, 2, :], op0=ALU.add, op1=ALU.mult), r=['mod', 'normg'], w=['der'])
                    sch.op('dve', lambda c=c: nc.vector.tensor_tensor(out=der[:, l, 3, :, c], in0=mod[:, l, 40:48, c], in1=normg[:, l, 3, :], op=ALU.mult), r=['mod', 'normg'], w=['der'])
            sch.barrier()
            es_a.close()
            dump('mod', mod[:], 'mod')
            dump('der', der[:], 'der')
            if stop_after == 'ada':
                return

            def norm_to_hA(xbuf, xkey, c, l, gi, shoff, sq8, rs, t1):
                cond = 0 if c < 4 else 1
                sch.op('act', lambda: nc.scalar.activation(out=sq8[:], in_=xbuf, func=AF.Square), r=[xkey], w=['sq8'])
                ss, sskey = psbank()
                sch.op('pe', mm_group(ss[:, :], [(onesb[:], sq8[:, k, :]) for k in range(8)]), r=['sq8', 'onesb'], w=[sskey])
                sch.op('act', lambda: nc.scalar.activation(out=rs[:], in_=ss[:, :], func=AF.Sqrt, bias=EPS, scale=1.0 / D), r=[sskey], w=['rs'])
                sch.op('dve', lambda: nc.vector.reciprocal(out=rs[:], in_=rs[:]), r=['rs'], w=['rs'])
                for k in range(8):
                    tk = ('t1', k % 2)
                    sch.op('dve', lambda k=k: nc.vector.tensor_tensor(out=t1[:, k % 2, :], in0=xbuf[:, k, :], in1=rs[:], op=ALU.mult), r=[xkey, 'rs'], w=[tk])
                    sch.op('act', lambda k=k: nc.scalar.activation(out=hA[:, k, c * 512:(c + 1) * 512], in_=t1[:, k % 2, :], func=AF.Identity,
                                                                  scale=der[:, l, gi, k, cond:cond + 1], bias=mod[:, l, shoff + k, cond:cond + 1]),
                           r=[tk, 'der', 'mod'], w=[('hA', c)])

            def epilogue(src, srckey, c, l, ggi, x_src, x_src_key, x_dst, x_dst_key, nxt, xin, sq8, rs, t1):
                cond = 0 if c < 4 else 1
                xb = xin[c % len(xin)]
                xkey = ('xin', c % len(xin))
                sch.dma('sp', [(xb[:], x_src[:, :, c * 512:(c + 1) * 512])], r=[(x_src_key, c)], w=[xkey], slot=xkey)
                sch.op('act', lambda: nc.scalar.activation(out=sq8[:], in_=src, func=AF.Square), r=[srckey], w=['sq8'])
                ss, sskey = psbank()
                sch.op('pe', mm_group(ss[:, :], [(onesb[:], sq8[:, k, :]) for k in range(8)]), r=['sq8', 'onesb'], w=[sskey])
                sch.op('act', lambda: nc.scalar.activation(out=rs[:], in_=ss[:, :], func=AF.Sqrt, bias=EPS, scale=1.0 / D), r=[sskey], w=['rs'])
                sch.op('dve', lambda: nc.vector.reciprocal(out=rs[:], in_=rs[:]), r=['rs'], w=['rs'])
                for k in range(8):
                    tk = ('t1', k % 2)
                    sch.op('dve', lambda k=k: nc.vector.tensor_tensor(out=t1[:, k % 2, :], in0=src[:, k, :], in1=rs[:], op=ALU.mult), r=[srckey, 'rs'], w=[tk])
                    sch.op('dve', lambda k=k: nc.vector.scalar_tensor_tensor(out=xb[:, k, :], in0=t1[:, k % 2, :], scalar=der[:, l, ggi, k, cond:cond + 1], in1=xb[:, k, :], op0=ALU.mult, op1=ALU.add),
                           r=[tk, 'der', xkey], w=[xkey])
                sch.dma('sp', [(x_dst[:, :, c * 512:(c + 1) * 512], xb[:])], r=[xkey], w=[(x_dst_key, c)], slot=('xst', c % len(xin)))
                if nxt is not None:
                    nl, gi, shoff = nxt
                    norm_to_hA(xb[:], xkey, c, nl, gi, shoff, sq8, rs, t1)

            def dense_fm(slab_view_fn, skey, ct_list, consume):
                for ct in ct_list:
                    for c in range(NCH):
                        ps, pkey = psbank()
                        sch.op('pe', mm_group(ps[:, :], [(slab_view_fn(ct, k), hA[:, k, c * 512:(c + 1) * 512]) for k in range(8)]),
                               r=[skey, ('hA', c)], w=[pkey])
                        consume(ct, c, ps, pkey)

            for l in range(2):
                if l == 0:
                    with contextlib.ExitStack() as es:
                        xin = [es.enter_context(nc.sbuf_tensor(_un(f"xin{i}"), [128, 8, 512], F32)) for i in range(2)]
                        sq8 = es.enter_context(nc.sbuf_tensor(_un("sq8"), [128, 8, 512], BF16))
                        rs = es.enter_context(nc.sbuf_tensor(_un("rs"), [128, 512], F32))
                        t1 = es.enter_context(nc.sbuf_tensor(_un("t1"), [128, 2, 512], F32))
                        for c in range(NCH):
                            xb = xin[c % 2]
                            xkey = ('xin', c % 2)
                            sch.dma('sp', [(xb[:], xT_v[:, :, c * 512:(c + 1) * 512])], w=[xkey], slot=xkey)
                            norm_to_hA(xb[:], xkey, c, 0, 0, 0, sq8, rs, t1)
                        sch.barrier()
                    dump('hA0', hA[:], ('hA', 0))
                    if stop_after == 'n1':
                        return

                x_src = xT_v if l == 0 else xs_v
                x_src_key = 'xT' if l == 0 else 'xs'

                with contextlib.ExitStack() as es_og:
                    og = es_og.enter_context(nc.sbuf_tensor(_un("og"), [128, 8, T], BF16))
                    stopped = (gdn_mixer if l == 0 else hgrn_mixer)(nc, sch, dict(CTX, og=og))
                    if stopped:
                        return
                    sch.barrier()
                    if stop_after == f'mix{l}':
                        dump('og', og[:], ('og', 0))
                        return

                    with contextlib.ExitStack() as es:
                        xin = [es.enter_context(nc.sbuf_tensor(_un(f"xin{i}"), [128, 8, 512], F32)) for i in range(2)]
                        sq8 = es.enter_context(nc.sbuf_tensor(_un("sq8"), [128, 8, 512], BF16))
                        rs = es.enter_context(nc.sbuf_tensor(_un("rs"), [128, 512], F32))
                        t1 = es.enter_context(nc.sbuf_tensor(_un("t1"), [128, 2, 512], F32))
                        srcb = es.enter_context(nc.sbuf_tensor(_un("srcb"), [128, 8, 512], F32))
                        slab, skey = next_slab()
                        sv = slab[:, 0:8192].rearrange("p (k n) -> p k n", k=8)
                        wout = gdn_w_out if l == 0 else hgrn_w_out
                        sch.dma('pool', [(sv, wout.rearrange("(k p) n -> p k n", p=128))], w=[skey], slot=skey)
                        for c in range(NCH):
                            for m in range(8):
                                ps, pkey = psbank()
                                sch.op('pe', mm_group(ps[:, :], [(sv[:, k, m * 128:(m + 1) * 128], og[:, k, c * 512:(c + 1) * 512]) for k in range(8)]),
                                       r=[skey, ('og', c)], w=[pkey])
                                sch.op('act', lambda m=m, ps=ps: nc.scalar.copy(out=srcb[:, m, :], in_=ps[:, :]), r=[pkey], w=['srcb'])
                            epilogue(srcb[:], 'srcb', c, l, 1, x_src, x_src_key, xs_v, 'xs', (l, 2, 24), xin, sq8, rs, t1)
                        sch.barrier()
                if stop_after == f'epi1_{l}':
                    dump('hA0', hA[:], ('hA', 0))
                    return

                with contextlib.ExitStack() as es:
                    ffacc = es.enter_context(nc.sbuf_tensor(_un("ffacc"), [128, 8, T], F32))
                    hid = es.enter_context(nc.sbuf_tensor(_un("hid"), [128, 2, 4, 512], BF16))
                    hr = es.enter_context(nc.sbuf_tensor(_un("hr"), [128, 2, 512], F32))
                    xin = [es.enter_context(nc.sbuf_tensor(_un(f"xin{i}"), [128, 8, 512], F32)) for i in range(1)]
                    sq8 = es.enter_context(nc.sbuf_tensor(_un("sq8"), [128, 8, 512], BF16))
                    rs = es.enter_context(nc.sbuf_tensor(_un("rs"), [128, 512], F32))
                    t1 = es.enter_context(nc.sbuf_tensor(_un("t1"), [128, 2, 512], F32))
                    nhr = 0
                    for g in range(8):
                        slab, skey = next_slab()
                        w1v = slab[:, 0:4096].rearrange("p (k n) -> p k n", k=8)
                        w2v = slab[:, 4096:8192].rearrange("p (j n) -> p j n", j=4)
                        sch.dma('pool', [(w1v, mlp_w1[l, :, g * 512:(g + 1) * 512].rearrange("(k p) n -> p k n", p=128)),
                                         (w2v, mlp_w2[l, g * 512:(g + 1) * 512, :].rearrange("(j p) n -> p j n", p=128))],
                                w=[skey], slot=skey)
                        for c in range(NCH):
                            hb = (g * NCH + c) % 2
                            for j in range(4):
                                ps, pkey = psbank()
                                sch.op('pe', mm_group(ps[:, :], [(w1v[:, k, j * 128:(j + 1) * 128], hA[:, k, c * 512:(c + 1) * 512]) for k in range(8)]),
                                       r=[skey, ('hA', c)], w=[pkey])
                                hk = ('hr', nhr % 2)
                                sch.op('act', lambda ps=ps, i=nhr % 2: nc.scalar.activation(out=hr[:, i, :], in_=ps[:, :], func=AF.Relu), r=[pkey], w=[hk])
                                sch.op('dve', lambda i=nhr % 2, hb=hb, j=j: nc.vector.tensor_tensor(out=hid[:, hb, j, :], in0=hr[:, i, :], in1=hr[:, i, :], op=ALU.mult),
                                       r=[hk], w=[('hid', hb)])
                                nhr += 1
                            for m in range(8):
                                ps, pkey = psbank()
                                sch.op('pe', mm_group(ps[:, :], [(w2v[:, j, m * 128:(m + 1) * 128], hid[:, hb, j, :]) for j in range(4)]),
                                       r=[skey, ('hid', hb)], w=[pkey])
                                if g == 0:
                                    sch.op('act', lambda ps=ps, m=m, c=c: nc.scalar.copy(out=ffacc[:, m, c * 512:(c + 1) * 512], in_=ps[:, :]), r=[pkey], w=[('ff', c)])
                                else:
                                    sch.op('dve', lambda ps=ps, m=m, c=c: nc.vector.tensor_tensor(out=ffacc[:, m, c * 512:(c + 1) * 512], in0=ps[:, :], in1=ffacc[:, m, c * 512:(c + 1) * 512], op=ALU.add),
                                           r=[pkey, ('ff', c)], w=[('ff', c)])
                    for c in range(NCH):
                        if l == 0:
                            epilogue(ffacc[:, :, c * 512:(c + 1) * 512], ('ff', c), c, l, 3, xs_v, 'xs', xs_v, 'xs', (1, 0, 0), xin, sq8, rs, t1)
                        else:
                            epilogue(ffacc[:, :, c * 512:(c + 1) * 512], ('ff', c), c, l, 3, xs_v, 'xs', yT_v, 'yT', None, xin, sq8, rs, t1)
                    sch.barrier()
                if stop_after == f'layer{l}':
                    dump('hA0', hA[:], ('hA', 0))
                    return
        CTX = dict(hA=hA, consts=consts, onesb=onesb, identb=identb, psbank=psbank, psq=psq, psh=psh, next_slab=next_slab,
                   mm_group=mm_group, cst=cst, dump=dump, pst=pst, psb_=psb_, gdn_w_in=gdn_w_in, conv_wT=conv_wT, alog10=alog10,
                   dtb10=dtb10, gdn_on=gdn_on, sg0=sg0, sgo=sgo, hgrn_w_in=hgrn_w_in, lblT=lblT, hgrn_on=hgrn_on, sh0=sh0, sho=sho,
                   stop_after=stop_after)
        program()
        sch.final()
    return nc


def gdn_mixer(nc, sch, L):
    hA, og, consts, onesb, identb = L['hA'], L['og'], L['consts'], L['onesb'], L['identb']
    psbank, psq, psh, next_slab, mm_group, cst = L['psbank'], L['psq'], L['psh'], L['next_slab'], L['mm_group'], L['cst']
    dump, pst = L['dump'], L['pst']
    gdn_w_in, conv_wT, alog10, dtb10, gdn_on, sg0, sgo = L['gdn_w_in'], L['conv_wT'], L['alog10'], L['dtb10'], L['gdn_on'], L['sg0'], L['sgo']
    stop_after = L['stop_after']
    SCALE = 128 ** -0.5
    with contextlib.ExitStack() as es:
        def sb(name, shape, dt):
            return es.enter_context(nc.sbuf_tensor(_un(name), list(shape), dt))
        wbg = sb("wbg", [128, 8, 32], BF16)
        cw = sb("cw", [128, 24, 5], F32)
        al = sb("al", [128, 160], F32)
        dtb = sb("dtb", [128, 160], F32)
        onv = sb("onv", [128, 1], F32)
        beta = sb("beta", [128, NTL, 16], F32)
        gg = sb("gg", [128, NTL, 16], F32)
        gc = sb("gc", [128, NTL, 16], F32)
        kts = sb("kts", [128, NTL, 16], F32)
        tA = sb("tA", [128, 160], F32)
        preS = [sb(f"preS{i}", [128, 544], F32) for i in range(2)]
        preP = sb("preP", [128, 520], F32)
        cv = sb("cv", [128, 2, 512], F32)
        sl = sb("sl", [128, 512], F32)
        sqh = sb("sqh", [128, 512], BF16)
        rsh = sb("rsh", [128, 512], F32)
        qT = sb("qT", [128, T], BF16)
        kT = sb("kT", [128, T], BF16)
        vT = sb("vT", [128, T], BF16)
        ktok = sb("ktok", [128, NTL, 128], BF16)
        vtok = sb("vtok", [128, NTL, 128], BF16)
        zs = sb("zs", [128, T], BF16)
        oacc = sb("oacc", [128, T], F32)
        R2 = int(os.environ.get('R2', '2'))
        R1 = 1
        eGb = [sb(f"eGb{i}", [128, 2, 128], F32) for i in range(R2)]
        Dm = [sb(f"Dm{i}", [128, 2, 128], F32) for i in range(R1)]
        MFg = Dm
        dec = Dm
        decI = [sb(f"decI{i}", [128, 2, 128], F32) for i in range(R1)]
        decS = [sb(f"decS{i}", [128, 2, 128], F32) for i in range(R1)]
        Pm = [[sb(f"Pm{i}_{j}", [128, 2, 128], F32) for j in range(2)] for i in range(R1)]
        PTm = [[sb(f"PTm{i}_{j}", [128, 2, 128], F32) for j in range(2)] for i in range(R1)]
        Xm = [[sb(f"Xm{i}_{j}", [128, 2, 128], F32) for j in range(2)] for i in range(R1)]
        Xbf = [sb(f"Xbf{i}", [128, 2, 128], BF16) for i in range(R2)]
        qkT = [sb(f"qkT{i}", [128, 2, 128], BF16) for i in range(R2)]
        kgT = [sb(f"kgT{i}", [128, 2, 128], BF16) for i in range(R2)]
        qdT = [sb(f"qdT{i}", [128, 2, 128], BF16) for i in range(R2)]
        ktl = [sb(f"ktl{i}", [128, 2, 128], BF16) for i in range(R2)]
        S32 = [sb(f"S32_{d}", [128, 128], F32) for d in range(2)]
        Sb = [sb(f"Sb_{d}", [128, 128], BF16) for d in range(2)]
        vnew = [[sb(f"vnew{d}_{j}", [128, 128], BF16) for j in range(2)] for d in range(2)]
        rr = [sb(f"rr{d}", [128, 128], BF16) for d in range(2)]
        nbeta = sb("nbeta", [128, NTL, 16], F32)
        MFgb = sb("MFgb", [128, 2, 128], BF16)

        sch.dma('pool', [(wbg[:], gdn_w_in[:, 4096:4128].rearrange("(k p) n -> p k n", p=128))], w=['wbg'], slot='wbg')
        if os.environ.get('GS', '0') == '1':
            sch.dma('sp', [(cw[:], conv_wT), (al[:], alog10), (dtb[:], dtb10), (onv[:], gdn_on)], w=['cw', 'al', 'dtb', 'onv'], slot='gsetup')
        else:
            sch.dma('sp', [(cw[:], conv_wT)], w=['cw'], slot='gs0')
            sch.dma('sp', [(al[:], alog10)], w=['al'], slot='gs1')
            sch.dma('sp', [(dtb[:], dtb10)], w=['dtb'], slot='gs2')
            sch.dma('sp', [(onv[:], gdn_on)], w=['onv'], slot='gs3')
        sch.op('act', lambda: nc.scalar.activation(out=al[:], in_=al[:], func=AF.Exp), r=['al'], w=['al'])
        sch.op('dve', lambda: nc.vector.tensor_scalar(out=al[:], in0=al[:], scalar1=-1.0, scalar2=None, op0=ALU.mult), r=['al'], w=['al'])
        for d in range(2):
            for j in range(2):
                sch.op('dve', lambda d=d, j=j: nc.vector.memset(vnew[d][j][:], 0.0), w=[('vnew', d, j)])
            sch.op('dve', lambda d=d: nc.vector.memset(rr[d][:], 0.0), w=[('rr', d)])
        for i in range(2):
            sch.op('dve', lambda i=i: nc.vector.memset(preS[i][:], 0.0), w=[('preS', i)])
        sch.op('dve', lambda: nc.vector.memset(preP[:], 0.0), w=['preP'])

        for b in range(2):
            ps, pkey = psbank()
            pv = ps[:, 0:320].rearrange("p (i n) -> p i n", n=32)
            for ii in range(10):
                i = b * 10 + ii
                sch.op('pe', mm_group(pv[:, ii, :], [(hA[:, k, i * 128:(i + 1) * 128], wbg[:, k, :]) for k in range(8)]),
                       r=[('hA', i // 4), 'wbg'], w=[pkey])
            sch.op('act', lambda pv=pv, b=b: nc.scalar.activation(out=beta[:, b * 10:(b + 1) * 10, :], in_=pv[:, :, 0:16], func=AF.Sigmoid), r=[pkey], w=['beta'])
            tAv = tA[:, :].rearrange("p (i n) -> p i n", n=16)
            sch.op('dve', lambda pv=pv: nc.vector.tensor_tensor(out=tAv, in0=pv[:, :, 16:32], in1=dtb[:, :].rearrange("p (i n) -> p i n", n=16), op=ALU.add), r=[pkey, 'dtb'], w=['tA'])
            sch.op('act', lambda: nc.scalar.activation(out=tA[:], in_=tA[:], func=AF.Exp), r=['tA'], w=['tA'])
            sch.op('act', lambda: nc.scalar.activation(out=tA[:], in_=tA[:], func=AF.Ln, bias=1.0, scale=1.0), r=['tA'], w=['tA'])
            sch.op('dve', lambda b=b: nc.vector.tensor_tensor(out=gg[:, b * 10:(b + 1) * 10, :], in0=tAv, in1=al[:, :].rearrange("p (i n) -> p i n", n=16), op=ALU.mult), r=['tA', 'al'], w=['gg'])
        sch.op('dve', lambda: nc.vector.tensor_scalar(out=nbeta[:], in0=beta[:], scalar1=-1.0, scalar2=None, op0=ALU.mult), r=['beta'], w=['nbeta'])
        for b in range(2):
            ps, pkey = psbank()
            pv = ps[:, 0:320].rearrange("p (i n) -> p i n", n=32)
            for ii in range(10):
                i = b * 10 + ii
                sch.op('pe', lambda i=i, ii=ii, pv=pv: nc.tensor.matmul(pv[:, ii, 0:8], lhsT=cst(C_MF64), rhs=gg[:, i, 0:8], start=True, stop=True), r=['gg', 'consts'], w=[pkey])
                sch.op('pe', lambda i=i, ii=ii, pv=pv: nc.tensor.matmul(pv[:, ii, 8:16], lhsT=cst(C_MB64), rhs=gg[:, i, 8:16], start=True, stop=True), r=['gg', 'consts'], w=[pkey])
                sch.op('pe', lambda i=i, ii=ii, pv=pv: nc.tensor.matmul(pv[:, ii, 16:32], lhsT=cst(C_BLK64), rhs=gg[:, i, 0:16], start=True, stop=True), r=['gg', 'consts'], w=[pkey])
            sch.op('act', lambda pv=pv, b=b: nc.scalar.copy(out=gc[:, b * 10:(b + 1) * 10, :], in_=pv[:, :, 0:16]), r=[pkey], w=['gc'])
            sch.op('dve', lambda pv=pv, b=b: nc.vector.tensor_tensor(out=kts[:, b * 10:(b + 1) * 10, :], in0=pv[:, :, 16:32], in1=gc[:, b * 10:(b + 1) * 10, :], op=ALU.subtract), r=[pkey, 'gc'], w=['kts'])
        sch.op('act', lambda: nc.scalar.activation(out=kts[:], in_=kts[:], func=AF.Exp), r=['kts'], w=['kts'])
        dump('beta', beta[:], 'beta')
        dump('gg', gg[:], 'gg')
        dump('gc', gc[:], 'gc')
        dump('kts', kts[:], 'kts')
        if stop_after == 'g0':
            return True

        nq = [0]
        for h in range(8):
            slab, skey = next_slab()
            sv = slab[:, 0:4096].rearrange("p (c k n) -> p c k n", c=4, k=8)
            sch.dma('pool', [(sv[:, ci], gdn_w_in[:, off:off + 128].rearrange("(k p) n -> p k n", p=128))
                             for ci, off in enumerate([h * 128, 1024 + h * 128, 2048 + h * 128, 3072 + h * 128])],
                    w=[skey], slot=skey)
            for ct in [int(x) for x in os.environ.get('CTS', '0,1,2,3').split(',')]:
                for c in range(NCH):
                    ps, pkey = psbank()
                    sch.op('pe', mm_group(ps[:, :], [(sv[:, ct, k, :], hA[:, k, c * 512:(c + 1) * 512]) for k in range(8)]),
                           r=[skey, ('hA', c)], w=[pkey])
                    csl = slice(c * 512, (c + 1) * 512)
                    if ct == 3:
                        sch.op('act', lambda ps=ps, csl=csl: nc.scalar.activation(out=zs[:, csl], in_=ps[:, :], func=AF.Silu), r=[pkey], w=['zs'])
                        continue
                    i2 = nq[0] % 2
                    nq[0] += 1
                    if c < 4:
                        pre, prek = preS[i2], ('preS', i2)
                        p3 = pre[:, 0:544].rearrange("p (r n) -> p r n", n=68)
                        W = 64
                        psv = ps[:, :].rearrange("p (r n) -> p r n", n=64)
                    else:
                        pre, prek = preP, 'preP'
                        p3 = pre[:, 0:520].rearrange("p (r n) -> p r n", n=260)
                        W = 256
                        psv = ps[:, :].rearrange("p (r n) -> p r n", n=256)
                    sch.op('act', lambda p3=p3, psv=psv, W=W: nc.scalar.copy(out=p3[:, :, 2:2 + W], in_=psv), r=[pkey], w=[prek])
                    cvv = cv[:, i2, :].rearrange("p (r n) -> p r n", n=W)
                    cvk = ('cv', i2)
                    wt = ct * 8 + h
                    sch.op('dve', lambda p3=p3, cvv=cvv, W=W, wt=wt: nc.vector.tensor_scalar(out=cvv, in0=p3[:, :, 0:W], scalar1=cw[:, wt, 0:1], scalar2=None, op0=ALU.mult), r=[prek, 'cw'], w=[cvk])
                    for j in range(1, 5):
                        sch.op('dve', lambda p3=p3, cvv=cvv, W=W, wt=wt, j=j: nc.vector.scalar_tensor_tensor(out=cvv, in0=p3[:, :, j:j + W], scalar=cw[:, wt, j:j + 1], in1=cvv, op0=ALU.mult, op1=ALU.add), r=[prek, 'cw', cvk], w=[cvk])
                    if ct == 2:
                        sch.op('act', lambda i2=i2, csl=csl: nc.scalar.activation(out=vT[:, csl], in_=cv[:, i2, :], func=AF.Silu), r=[cvk], w=['vT'])
                        continue
                    slk = 'sl'
                    sch.op('act', lambda i2=i2: nc.scalar.activation(out=sl[:, :], in_=cv[:, i2, :], func=AF.Silu), r=[cvk], w=[slk])
                    NOL2 = int(os.environ.get('NOL2', '0'))
                    if NOL2 == 1:
                        continue
                    sch.op('act', lambda i2=i2: nc.scalar.activation(out=sqh[:, :], in_=sl[:, :], func=AF.Square), r=[slk], w=['sqh'])
                    if NOL2 == 2:
                        continue
                    ss, sskey = psbank()
                    sch.op('pe', lambda ss=ss, i2=i2: nc.tensor.matmul(ss[:, :], lhsT=onesb[:], rhs=sqh[:, :], start=True, stop=True), r=['sqh', 'onesb'], w=[sskey])
                    if NOL2 == 3:
                        continue
                    sch.op('act', lambda ss=ss, i2=i2: nc.scalar.activation(out=rsh[:, :], in_=ss[:, :], func=AF.Sqrt, bias=EPS, scale=1.0), r=[sskey], w=['rsh'])
                    if NOL2 == 4:
                        continue
                    sch.op('dve', lambda i2=i2: nc.vector.reciprocal(out=rsh[:, :], in_=rsh[:, :]), r=['rsh'], w=['rsh'])
                    dst = qT if ct == 0 else kT
                    dk = 'qT' if ct == 0 else 'kT'
                    sch.op('dve', lambda i2=i2, dst=dst, csl=csl: nc.vector.tensor_tensor(out=dst[:, csl], in0=sl[:, :], in1=rsh[:, :], op=ALU.mult), r=[slk, 'rsh'], w=[dk])
            if stop_after == 'g1a':
                dump('qT', qT[:], 'qT')
                dump('kT', kT[:], 'kT')
                dump('vT', vT[:], 'vT')
                dump('zs', zs[:], 'zs')
                return True
            for src, sk, dstt, dk in ((kT, 'kT', ktok, 'ktok'), (vT, 'vT', vtok, 'vtok')):
                for i4 in range(int(os.environ.get('NTR', '5'))):
                    tb, tbk = psbank()
                    tv = tb[:, 0:512].rearrange("p (i n) -> p i n", n=128)
                    for ii in range(4):
                        i = i4 * 4 + ii
                        sch.op('pe', lambda src=src, i=i, ii=ii, tv=tv: nc.tensor.matmul(tv[:, ii, :], lhsT=src[:, i * 128:(i + 1) * 128], rhs=identb[:], start=True, stop=True), r=[sk, 'identb'], w=[tbk])
                    sch.op('act', lambda dstt=dstt, i4=i4, tv=tv: nc.scalar.copy(out=dstt[:, i4 * 4:(i4 + 1) * 4, :], in_=tv), r=[tbk], w=[dk])
            if h == 0:
                dump('qT', qT[:], 'qT')
                dump('kT', kT[:], 'kT')
                dump('vT', vT[:], 'vT')
                dump('ktok', ktok[:], 'ktok')
                dump('zs', zs[:], 'zs')
                if stop_after == 'g1':
                    return True
            sch.op('dve', lambda: nc.vector.memset(oacc[:], 0.0), w=['oacc'])

            npair = 0
            for si, (t0, n) in enumerate(SEQS):
                if si == 0:
                    for d in range(2):
                        sch.dma('sp', [(S32[d][:], sg0[d, h])], w=[('S32', d)], slot=('S32', d))
                for d in range(2):
                    if si != 0:
                        sch.op('dve', lambda d=d: nc.vector.memset(S32[d][:], 0.0), w=[('S32', d)])
                    sch.op('act', lambda d=d: nc.scalar.copy(out=Sb[d][:], in_=S32[d][:]), r=[('S32', d)], w=[('Sb', d)])
                for p in range(n):
                    ri = npair % R2
                    npair += 1
                    tiles = [t0 + p, t0 + n - 1 - p]
                    cols = [h, 8 + h]
                    tsl = [slice(i * 128, (i + 1) * 128) for i in tiles]
                    K = lambda name: (name, ri)
                    rj = 0
                    KJ = lambda name: (name, rj)
                    _cutn = int(os.environ.get('CUT', '99'))
                    _cnt = [0]

                    def bop(*a, **k):
                        if p >= 1:
                            _cnt[0] += 1
                            if _cnt[0] > _cutn:
                                return
                        sch.op(*a, **k)
                    GBBF = os.environ.get('GBBF', '0') == '1'
                    for d in range(2):
                        bop('dve', lambda d=d: nc.vector.tensor_scalar(out=(MFgb if GBBF else MFg[rj])[:, d, :], in0=cst(C_MF64 if d == 0 else C_MB64), scalar1=gg[:, tiles[d], cols[d]:cols[d] + 1], scalar2=None, op0=ALU.mult),
                               r=['consts', 'gg'], w=[KJ('Dm'), 'MFgb'])
                    gb, gbk = psh()
                    for d in range(2):
                        if GBBF:
                            bop('pe', lambda d=d, gb=gb: nc.tensor.matmul(gb[:, d * 128:(d + 1) * 128], lhsT=onesb[:], rhs=MFgb[:, d, :], start=True, stop=True), r=['onesb', 'MFgb'], w=gbk)
                        else:
                            bop('pe', lambda d=d, gb=gb: nc.tensor.matmul(gb[:, d * 128:(d + 1) * 128], lhsT=cst(C_ONES), rhs=MFg[rj][:, d, :], start=True, stop=True), r=['consts', KJ('Dm')], w=gbk)
                    bop('act', lambda gb=gb: nc.scalar.activation(out=eGb[ri][:, :, :].rearrange("p a b -> p (a b)"), in_=gb, func=AF.Exp), r=gbk, w=[K('eGb')])
                    for d in range(2):
                        bop('dve', lambda d=d, gb=gb: nc.vector.tensor_scalar(out=Dm[rj][:, d, :], in0=gb[:, d * 128:(d + 1) * 128], scalar1=gc[:, tiles[d], cols[d]:cols[d] + 1], scalar2=0.0, op0=ALU.subtract, op1=ALU.min),
                               r=gbk + ['gc', K('eGb')], w=[KJ('Dm')])
                    bop('act', lambda: nc.scalar.activation(out=dec[rj][:], in_=Dm[rj][:], func=AF.Exp), r=[KJ('Dm')], w=[KJ('Dm')])
                    for d in range(2):
                        bop('dve', lambda d=d: nc.vector.tensor_tensor(out=decI[rj][:, d, :], in0=dec[rj][:, d, :], in1=cst(C_MF64 if d == 0 else C_MB64), op=ALU.mult), r=[KJ('Dm'), 'consts'], w=[KJ('decI')])
                        bop('dve', lambda d=d: nc.vector.tensor_tensor(out=decS[rj][:, d, :], in0=dec[rj][:, d, :], in1=cst(C_SF64 if d == 0 else C_SB64), op=ALU.mult), r=[KJ('Dm'), 'consts'], w=[KJ('decS')])
                    SS = int(os.environ.get('SCAN_STOP', '0'))
                    if SS == 1 and p == int(os.environ.get('SSP', '0')):
                        dump('oacc', oacc[:], 'oacc')
                        return True
                    kk, kkk = psh()
                    qk, qkk = psh()
                    for d in range(2):
                        sch.op('pe', lambda d=d, kk=kk: nc.tensor.matmul(kk[:, d * 128:(d + 1) * 128], lhsT=kT[:, tsl[d]], rhs=kT[:, tsl[d]], start=True, stop=True), r=['kT'], w=kkk)
                        sch.op('pe', lambda d=d, qk=qk: nc.tensor.matmul(qk[:, d * 128:(d + 1) * 128], lhsT=kT[:, tsl[d]], rhs=qT[:, tsl[d]], start=True, stop=True), r=['kT', 'qT'], w=qkk)
                    P0, PT0, X0 = Pm[rj][0], PTm[rj][0], Xm[rj][0]
                    for d in range(2):
                        sch.op('dve', lambda d=d, kk=kk: nc.vector.scalar_tensor_tensor(out=P0[:, d, :], in0=kk[:, d * 128:(d + 1) * 128], scalar=nbeta[:, tiles[d], cols[d]:cols[d] + 1], in1=decS[rj][:, d, :], op0=ALU.mult, op1=ALU.mult),
                               r=kkk + ['nbeta', KJ('decS')], w=[KJ('P0')])
                        sch.op('dve', lambda d=d, qk=qk: nc.vector.scalar_tensor_tensor(out=qkT[ri][:, d, :], in0=qk[:, d * 128:(d + 1) * 128], scalar=SCALE, in1=decI[rj][:, d, :], op0=ALU.mult, op1=ALU.mult),
                               r=qkk + [KJ('decI')], w=[K('qkT')])
                    tb, tbk = psh()
                    tv = tb.rearrange("p (i n) -> p i n", n=128)
                    for d in range(2):
                        sch.op('pe', lambda d=d, tv=tv: nc.tensor.matmul(tv[:, d, :], lhsT=P0[:, d, :], rhs=cst(C_ID), start=True, stop=True), r=[KJ('P0'), 'consts'], w=tbk)
                    sch.op('act', lambda tv=tv: nc.scalar.copy(out=PT0[:], in_=tv), r=tbk, w=[KJ('PT0')])
                    for d in range(2):
                        sch.op('dve', lambda d=d: nc.vector.tensor_tensor(out=X0[:, d, :], in0=P0[:, d, :], in1=cst(C_ID), op=ALU.add), r=[KJ('P0'), 'consts'], w=[KJ('X0')])
                    if SS == 2 and p == int(os.environ.get('SSP', '0')):
                        dump('oacc', oacc[:], 'oacc')
                        return True
                    cur = 0
                    for lvl in range(1, 1 + int(os.environ.get('LVLS', '5'))):
                        nx = 1 - cur
                        Pc, PTc, Xc = Pm[rj][cur], PTm[rj][cur], Xm[rj][cur]
                        Pn, PTn, Xn = Pm[rj][nx], PTm[rj][nx], Xm[rj][nx]
                        kc = lambda nm, c_=cur: (nm + str(c_), rj)
                        kn = lambda nm, c_=nx: (nm + str(c_), rj)
                        _b, _k = psbank()
                        pt2, pt2k = _b[:, 0:256], [_k]
                        for d in range(2):
                            sch.op('pe', lambda d=d, pt2=pt2, Pc=Pc, PTc=PTc: nc.tensor.matmul(pt2[:, d * 128:(d + 1) * 128], lhsT=Pc[:, d, :], rhs=PTc[:, d, :], start=True, stop=True), r=[kc('P'), kc('PT')], w=pt2k)
                        sch.op('act', lambda pt2=pt2, PTn=PTn: nc.scalar.copy(out=PTn[:, :, :].rearrange("p a b -> p (a b)"), in_=pt2), r=pt2k, w=[kn('PT')])
                        if lvl < 5:
                            _b, _k = psbank()
                            p2, p2k = _b[:, 0:256], [_k]
                            for d in range(2):
                                sch.op('pe', lambda d=d, p2=p2, Pc=Pc, PTc=PTc: nc.tensor.matmul(p2[:, d * 128:(d + 1) * 128], lhsT=PTc[:, d, :], rhs=Pc[:, d, :], start=True, stop=True), r=[kc('P'), kc('PT')], w=p2k)
                            sch.op('act', lambda p2=p2, Pn=Pn: nc.scalar.copy(out=Pn[:, :, :].rearrange("p a b -> p (a b)"), in_=p2), r=p2k, w=[kn('P')])
                        _b, _k = psbank()
                        xp, xpk = _b[:, 0:256], [_k]
                        for d in range(2):
                            sch.op('pe', lambda d=d, xp=xp, PTn=PTn, Xc=Xc: nc.tensor.matmul(xp[:, d * 128:(d + 1) * 128], lhsT=PTn[:, d, :], rhs=Xc[:, d, :], start=True, stop=True), r=[kn('PT'), kc('X')], w=xpk)
                        sch.op('dve', lambda xp=xp, Xn=Xn, Xc=Xc: nc.vector.tensor_tensor(out=Xn[:, :, :].rearrange("p a b -> p (a b)"), in0=xp, in1=Xc[:, :, :].rearrange("p a b -> p (a b)"), op=ALU.add), r=xpk + [kc('X')], w=[kn('X')])
                        cur = nx
                    Xf32 = Xm[rj][cur]
                    sch.op('act', lambda Xf32=Xf32: nc.scalar.copy(out=Xbf[ri][:], in_=Xf32[:]), r=[('X' + str(cur), rj)], w=[('Xbf', ri)])
                    Xf = Xbf[ri]
                    Xk = ('Xbf', ri)
                    for d in range(2):
                        sch.op('dve', lambda d=d: nc.vector.tensor_tensor(out=kgT[ri][:, d, :], in0=kT[:, tsl[d]], in1=eGb[ri][:, d, :], op=ALU.mult), r=['kT', K('eGb')], w=[K('kgT')])
                        sch.op('dve', lambda d=d: nc.vector.scalar_tensor_tensor(out=qdT[ri][:, d, :], in0=qT[:, tsl[d]], scalar=SCALE, in1=eGb[ri][:, d, :], op0=ALU.mult, op1=ALU.mult), r=['qT', K('eGb')], w=[K('qdT')])
                        sch.op('dve', lambda d=d: nc.vector.tensor_scalar(out=ktl[ri][:, d, :], in0=ktok[:, tiles[d], :], scalar1=kts[:, tiles[d], cols[d]:cols[d] + 1], scalar2=None, op0=ALU.mult), r=['ktok', 'kts'], w=[K('ktl')])
                    if h == 0 and si == 0 and p == 0:
                        dump('X', Xf[:], Xk)
                        dump('qkT', qkT[ri][:], K('qkT'))
                        dump('eGb', eGb[ri][:], K('eGb'))
                        dump('P0', Pm[rj][0][:], KJ('P0'))
                    if SS == 3:
                        dump('oacc', oacc[:], 'oacc')
                        return True
                    ops_ = [(L['psb_'][OPSB][:, d * 128:(d + 1) * 128], ('ps', OPSB)) for d in range(2)]
                    for jj in range(0 if ((p >= 1 and os.environ.get('SKIPC2') == '1') or os.environ.get('SKIPC2') == '2') else 2):
                        for d in range(2):
                            j = jj if d == 0 else 1 - jj
                            cs = slice(64 * j, 64 * j + 64)
                            a_, ak = psq()
                            sch.op('pe', lambda d=d, a_=a_: nc.tensor.matmul(a_, lhsT=kgT[ri][:, d, :], rhs=Sb[d][:], start=True, stop=True), r=[K('kgT'), ('Sb', d)], w=[ak])
                            sch.op('dve', lambda d=d, a_=a_: nc.vector.tensor_tensor(out=rr[d][:], in0=vtok[:, tiles[d], :], in1=a_, op=ALU.subtract), r=['vtok', ak], w=[('rr', d)])
                            v_, vk = psq()
                            sch.op('pe', lambda d=d, v_=v_: nc.tensor.matmul(v_, lhsT=Xf[:, d, :], rhs=rr[d][:], start=True, stop=True), r=[Xk, ('rr', d)], w=[vk])
                            sch.op('act', lambda d=d, j=j, cs=cs, v_=v_: nc.scalar.activation(out=vnew[d][j][cs, :], in_=v_[cs, :], func=AF.Copy, scale=beta[cs, tiles[d], cols[d]:cols[d] + 1]),
                                   r=[vk, 'beta'], w=[('vnew', d, j)])
                            o_, ok_ = ops_[d]
                            sch.op('pe', lambda d=d, j=j, cs=cs, o_=o_: (nc.tensor.matmul(o_[:, cs], lhsT=Sb[d][:], rhs=qdT[ri][:, d, cs], start=True, stop=False),
                                                                 nc.tensor.matmul(o_[:, cs], lhsT=vnew[d][j][:], rhs=qkT[ri][:, d, cs], start=False, stop=True))[1],
                                   r=[('Sb', d), K('qdT'), ('vnew', d, j), K('qkT')], w=[ok_])
                            ds_, dsk = psq()
                            sch.op('pe', lambda d=d, j=j, ds_=ds_: nc.tensor.matmul(ds_, lhsT=ktl[ri][:, d, :], rhs=vnew[d][j][:], start=True, stop=True), r=[K('ktl'), ('vnew', d, j)], w=[dsk])
                            gl = 64 * j + 63 if d == 0 else 64 * j
                            sch.op('dve', lambda d=d, ds_=ds_, gl=gl: nc.vector.scalar_tensor_tensor(out=S32[d][:], in0=S32[d][:], scalar=eGb[ri][:, d, gl:gl + 1], in1=ds_, op0=ALU.mult, op1=ALU.add),
                                   r=[('S32', d), K('eGb'), dsk], w=[('S32', d)])
                            sch.op('act', lambda d=d: nc.scalar.copy(out=Sb[d][:], in_=S32[d][:]), r=[('S32', d)], w=[('Sb', d)])
                    for d in range(2):
                        if (p >= 1 and os.environ.get('SKIPC2') == '1') or os.environ.get('SKIPC2') == '2':
                            break
                        o_, ok_ = ops_[d]
                        sch.op('dve', lambda d=d, o_=o_: nc.vector.tensor_tensor(out=oacc[:, tsl[d]], in0=o_, in1=oacc[:, tsl[d]], op=ALU.add), r=[ok_, 'oacc'], w=['oacc'])
                    if BARR >= 1:
                        sch.barrier()
                    if SS == 4 or (SS == 6 and p + 1 >= int(os.environ.get('PMAX', '2'))):
                        dump('oacc', oacc[:], 'oacc')
                        return True
                if SS == 5:
                    dump('oacc', oacc[:], 'oacc')
                    return True
                if si > 0:
                    for d in range(2):
                        sch.dma('sp', [(sgo[si - 1, d, h], S32[d][:])], r=[('S32', d)], w=[('sgo', si, d, h)], slot=('sst', d))
            if h == 0:
                dump('oacc', oacc[:], 'oacc')
                if stop_after == 'g2':
                    return True
            finalize_head(nc, sch, L, oacc, zs, onv, og, h, sqh, rsh, sl)
        return False


def finalize_head(nc, sch, L, oacc, zs, onv, og, h, sqh, rsh, sl):
    psbank, onesb = L['psbank'], L['onesb']
    for c in range(NCH):
        i2 = c % 2
        csl = slice(c * 512, (c + 1) * 512)
        sch.op('act', lambda i2=i2, csl=csl: nc.scalar.activation(out=sqh[:, :], in_=oacc[:, csl], func=AF.Square), r=['oacc'], w=['sqh'])
        ss, sskey = psbank()
        sch.op('pe', lambda ss=ss, i2=i2: nc.tensor.matmul(ss[:, :], lhsT=onesb[:], rhs=sqh[:, :], start=True, stop=True), r=['sqh', 'onesb'], w=[sskey])
        sch.op('act', lambda ss=ss, i2=i2: nc.scalar.activation(out=rsh[:, :], in_=ss[:, :], func=AF.Sqrt, bias=EPS, scale=1.0 / 128), r=[sskey], w=['rsh'])
        sch.op('dve', lambda i2=i2: nc.vector.reciprocal(out=rsh[:, :], in_=rsh[:, :]), r=['rsh'], w=['rsh'])
        sch.op('dve', lambda i2=i2, csl=csl: nc.vector.tensor_tensor(out=sl[:, :], in0=oacc[:, csl], in1=rsh[:, :], op=ALU.mult), r=['oacc', 'rsh'], w=['sl'])
        sch.op('dve', lambda i2=i2, csl=csl: nc.vector.scalar_tensor_tensor(out=og[:, h, csl], in0=sl[:, :], scalar=onv[:, 0:1], in1=zs[:, csl], op0=ALU.mult, op1=ALU.mult),
               r=['sl', 'onv', 'zs'], w=[('og', c)])


def hgrn_mixer(nc, sch, L):
    hA, og, consts, onesb, identb = L['hA'], L['og'], L['consts'], L['onesb'], L['identb']
    psbank, next_slab, mm_group, cst, dump, psb_ = L['psbank'], L['next_slab'], L['mm_group'], L['cst'], L['dump'], L['psb_']
    hgrn_w_in, lblT, hgrn_on, sh0, sho = L['hgrn_w_in'], L['lblT'], L['hgrn_on'], L['sh0'], L['sho']
    stop_after = L['stop_after']
    with contextlib.ExitStack() as es:
        def sb(name, shape, dt):
            return es.enter_context(nc.sbuf_tensor(_un(name), list(shape), dt))
        lbl = sb("lbl", [128, 2, 16], F32)
        lbv = sb("lbv", [128, 16], F32)
        lbt = sb("lbt", [128, 16], F32)
        onv = sb("onv", [128, 1], F32)
        dl = sb("dl", [128, 256], F32)
        lbB = sb("lbB", [128, 256], F32)
        omB = sb("omB", [128, 256], F32)
        sg = [sb(f"sg{i}", [128, 256], F32) for i in range(2)]
        logf = sb("logf", [128, NTL, 2, 128], F32)
        ktok = sb("ktok", [128, NTL, 2, 128], BF16)
        vtok = sb("vtok", [128, NTL, 128], BF16)
        qsT = sb("qsT", [128, T], BF16)
        zs = sb("zs", [128, T], BF16)
        oacc = sb("oacc", [128, T], F32)
        sqh = sb("sqh", [128, 512], BF16)
        rsh = sb("rsh", [128, 512], F32)
        sl = sb("sl", [128, 512], F32)
        Vm = [sb(f"Vm{i}", [128, 8, 128], BF16) for i in range(2)]
        E2 = [[sb(f"E2_{i}_{d}", [128, 256], F32) for d in range(2)] for i in range(2)]
        enb = [[sb(f"enb_{i}_{d}", [128, 128], F32) for d in range(2)] for i in range(2)]
        fl = [[sb(f"fl_{i}_{d}", [128, 8], F32) for d in range(2)] for i in range(2)]
        qin = [[sb(f"qin_{i}_{d}", [128, 128], BF16) for d in range(2)] for i in range(2)]
        kout = [[sb(f"kout_{i}_{d}", [128, 128], BF16) for d in range(2)] for i in range(2)]
        ktl = [[sb(f"ktl_{i}_{d}", [128, 128], BF16) for d in range(2)] for i in range(2)]
        att = [[sb(f"att_{i}_{d}", [128, 128], BF16) for d in range(2)] for i in range(2)]
        S32 = [sb(f"S32_{d}", [128, 128], F32) for d in range(2)]
        Sb = [sb(f"Sb_{d}", [128, 128], BF16) for d in range(2)]

        sch.dma('sp', [(lbl[:], lblT)], w=['lbl'], slot='gs0')
        sch.dma('sp', [(onv[:], hgrn_on)], w=['onv'], slot='gs1')
        sch.op('act', lambda: nc.scalar.activation(out=lbl[:], in_=lbl[:], func=AF.Exp), r=['lbl'], w=['lbl'])
        sch.op('dve', lambda: nc.vector.tensor_tensor(out=lbt[:], in0=lbl[:, 0, :], in1=lbl[:, 1, :], op=ALU.add), r=['lbl'], w=['lbt'])
        sch.op('dve', lambda: nc.vector.reciprocal(out=lbt[:], in_=lbt[:]), r=['lbt'], w=['lbt'])
        sch.op('dve', lambda: nc.vector.tensor_tensor(out=lbv[:], in0=lbl[:, 1, :], in1=lbt[:], op=ALU.mult), r=['lbl', 'lbt'], w=['lbv'])

        for h in range(8):
            slab, skey = next_slab()
            sv = slab[:, 0:5120].rearrange("p (k c n) -> p k c n", k=8, c=5)
            offs = [h * 128, 4096 + h * 128, 1024 + h * 128, 2048 + h * 128, 3072 + h * 128]
            sch.dma('pool', [(sv[:, :, ci, :], hgrn_w_in[:, off:off + 128].rearrange("(k p) n -> p k n", p=128)) for ci, off in enumerate(offs)],
                    w=[skey], slot=skey)
            for d in range(2):
                sch.op('dve', lambda d=d: nc.vector.tensor_scalar(out=dl[:, d * 128:(d + 1) * 128], in0=cst(C_ID), scalar1=lbv[:, d * 8 + h:d * 8 + h + 1], scalar2=None, op0=ALU.mult),
                       r=['consts', 'lbv'], w=['dl'])
            pb, pbk = psbank()
            sch.op('pe', lambda pb=pb: nc.tensor.matmul(pb[:, 0:256], lhsT=cst(C_ONES), rhs=dl[:], start=True, stop=True), r=['consts', 'dl'], w=[pbk])
            sch.op('act', lambda pb=pb: nc.scalar.copy(out=lbB[:], in_=pb[:, 0:256]), r=[pbk], w=['lbB'])
            sch.op('dve', lambda: nc.vector.tensor_scalar(out=omB[:], in0=lbB[:], scalar1=-1.0, scalar2=1.0, op0=ALU.mult, op1=ALU.add), r=['lbB'], w=['omB'])
            for ct, (dst, dk) in enumerate(((qsT, 'qsT'), (zs, 'zs'))):
                for c in range(NCH):
                    ps, pkey = psbank()
                    sch.op('pe', mm_group(ps[:, :], [(sv[:, k, ct, :], hA[:, k, c * 512:(c + 1) * 512]) for k in range(8)]), r=[skey, ('hA', c)], w=[pkey])
                    sch.op('act', lambda ps=ps, dst=dst, c=c: nc.scalar.activation(out=dst[:, c * 512:(c + 1) * 512], in_=ps[:, :], func=AF.Silu), r=[pkey], w=[dk])
            for i in range(NTL):
                ps, pkey = psbank()
                sch.op('pe', mm_group(ps[:, 0:384], [(hA[:, k, i * 128:(i + 1) * 128], sv[:, k, 2:5, :].rearrange("p c n -> p (c n)")) for k in range(8)]),
                       r=[skey, ('hA', i // 4)], w=[pkey])
                sgi = sg[i % 2]
                sgk = ('sg', i % 2)
                sch.op('act', lambda ps=ps, sgi=sgi: nc.scalar.activation(out=sgi[:], in_=ps[:, 0:256], func=AF.Sigmoid), r=[pkey], w=[sgk])
                sch.op('act', lambda ps=ps, i=i: nc.scalar.copy(out=vtok[:, i, :], in_=ps[:, 256:384]), r=[pkey], w=['vtok'])
                sch.op('dve', lambda sgi=sgi: nc.vector.tensor_tensor(out=sgi[:], in0=sgi[:], in1=omB[:], op=ALU.mult), r=[sgk, 'omB'], w=[sgk])
                sch.op('dve', lambda sgi=sgi: nc.vector.tensor_tensor(out=sgi[:], in0=sgi[:], in1=lbB[:], op=ALU.add), r=[sgk, 'lbB'], w=[sgk])
                sch.op('act', lambda sgi=sgi, i=i: nc.scalar.activation(out=logf[:, i, :, :].rearrange("p a b -> p (a b)"), in_=sgi[:], func=AF.Ln), r=[sgk], w=['logf'])
                sch.op('dve', lambda sgi=sgi, i=i: nc.vector.tensor_scalar(out=ktok[:, i, :, :].rearrange("p a b -> p (a b)"), in0=sgi[:], scalar1=-1.0, scalar2=1.0, op0=ALU.mult, op1=ALU.add), r=[sgk], w=['ktok'])
            if h == 0:
                dump('qsT', qsT[:], 'qsT')
                dump('logf', logf[:], 'logf')
                dump('ktokh', ktok[:], 'ktok')
                dump('vtokh', vtok[:], 'vtok')
                if stop_after == 'h1':
                    return True
            sch.op('dve', lambda: nc.vector.memset(oacc[:], 0.0), w=['oacc'])

            npair = 0
            for si, (t0, n) in enumerate(SEQS):
                if si == 0:
                    for d in range(2):
                        sch.dma('sp', [(S32[d][:], sh0[d, h])], w=[('S32', d)], slot=('S32', d))
                for d in range(2):
                    if si != 0:
                        sch.op('dve', lambda d=d: nc.vector.memset(S32[d][:], 0.0), w=[('S32', d)])
                    sch.op('act', lambda d=d: nc.scalar.copy(out=Sb[d][:], in_=S32[d][:]), r=[('S32', d)], w=[('Sb', d)])
                for p in range(n):
                    ri = npair % 2
                    npair += 1
                    tiles = [t0 + p, t0 + n - 1 - p]
                    tsl = [slice(i * 128, (i + 1) * 128) for i in tiles]
                    vms = []
                    for d in range(2):
                        if d == 1 and tiles[1] == tiles[0]:
                            vms.append(vms[0])
                            continue
                        vi = (2 * npair + d) % 2
                        vmk = ('Vm', vi)
                        for nn in range(8):
                            if nn % 2 == 0:
                                sch.op('dve', lambda vi=vi, nn=nn, d=d: nc.vector.tensor_scalar(out=Vm[vi][:, nn, :], in0=vtok[:, tiles[d], :], scalar1=cst(C_CM)[:, nn:nn + 1], scalar2=None, op0=ALU.mult),
                                       r=['vtok', 'consts'], w=[vmk])
                            else:
                                sch.op('act', lambda vi=vi, nn=nn, d=d: nc.scalar.activation(out=Vm[vi][:, nn, :], in_=vtok[:, tiles[d], :], func=AF.Copy, scale=cst(C_CM)[:, nn:nn + 1]),
                                       r=['vtok', 'consts'], w=[vmk])
                        vms.append((Vm[vi], vmk))
                    ops_ = [(psb_[OPSB], ('ps', OPSB)), (psb_[OPSB + 1], ('ps', OPSB + 1))]
                    for d in range(2):
                        i = tiles[d]
                        K = lambda name, d=d: (name, ri, d)
                        Lg = logf[:, i, d, :]
                        cb, cbk = psbank()
                        sch.op('pe', lambda cb=cb, Lg=Lg, d=d: nc.tensor.matmul(cb[:, 0:128], lhsT=Lg, rhs=cst(C_MF16 if d == 0 else C_MB16), start=True, stop=True), r=['logf', 'consts'], w=[cbk])
                        sch.op('pe', lambda cb=cb, Lg=Lg, d=d: nc.tensor.matmul(cb[:, 128:256], lhsT=cst(C_SB16 if d == 0 else C_SF16), rhs=Lg, start=True, stop=True), r=['logf', 'consts'], w=[cbk])
                        sch.op('pe', lambda cb=cb, Lg=Lg: nc.tensor.matmul(cb[:, 256:264], lhsT=Lg, rhs=cst(C_CM)[:, 0:8], start=True, stop=True), r=['logf', 'consts'], w=[cbk])
                        sch.op('act', lambda cb=cb, d=d: nc.scalar.activation(out=E2[ri][d][:], in_=cb[:, 0:256], func=AF.Exp), r=[cbk], w=[K('E2')])
                        sch.op('act', lambda cb=cb, d=d: nc.scalar.activation(out=enb[ri][d][:], in_=cb[:, 0:128], func=AF.Exp, scale=-1.0), r=[cbk], w=[K('enb')])
                        sch.op('act', lambda cb=cb, d=d: nc.scalar.activation(out=fl[ri][d][:], in_=cb[:, 256:264], func=AF.Exp), r=[cbk], w=[K('fl')])
                        kt, ktk = psbank()
                        sch.op('pe', lambda kt=kt, i=i, d=d: nc.tensor.matmul(kt[:, 0:128], lhsT=ktok[:, i, d, :], rhs=identb[:], start=True, stop=True), r=['ktok', 'identb'], w=[ktk])
                        sch.op('dve', lambda d=d: nc.vector.tensor_tensor(out=qin[ri][d][:], in0=qsT[:, tsl[d]], in1=E2[ri][d][:, 0:128], op=ALU.mult), r=['qsT', K('E2')], w=[K('qin')])
                        sch.op('dve', lambda kt=kt, d=d: nc.vector.tensor_tensor(out=kout[ri][d][:], in0=kt[:, 0:128], in1=enb[ri][d][:], op=ALU.mult), r=[ktk, K('enb')], w=[K('kout')])
                        sch.op('dve', lambda i=i, d=d: nc.vector.tensor_tensor(out=ktl[ri][d][:], in0=ktok[:, i, d, :], in1=E2[ri][d][:, 128:256], op=ALU.mult), r=['ktok', K('E2')], w=[K('ktl')])
                        ap_, apk = psbank()
                        sch.op('pe', lambda ap_=ap_, d=d: nc.tensor.matmul(ap_[:, 0:128], lhsT=kout[ri][d][:], rhs=qin[ri][d][:], start=True, stop=True), r=[K('kout'), K('qin')], w=[apk])
                        sch.op('dve', lambda ap_=ap_, d=d: nc.vector.tensor_tensor(out=att[ri][d][:], in0=ap_[:, 0:128], in1=cst(C_MF16 if d == 0 else C_MB16), op=ALU.mult), r=[apk, 'consts'], w=[K('att')])
                        o_, ok_ = ops_[d]
                        sch.op('pe', lambda o_=o_, i=i, d=d: nc.tensor.matmul(o_[:, 0:128], lhsT=vtok[:, i, :], rhs=att[ri][d][:], start=True, stop=False, skip_group_check=True), r=['vtok', K('att')], w=[ok_])
                    for jj in range(8):
                        for d in range(2):
                            K = lambda name, d=d: (name, ri, d)
                            nn = jj if d == 0 else 7 - jj
                            cn = slice(16 * nn, 16 * nn + 16)
                            o_, ok_ = ops_[d]
                            sch.op('pe', lambda o_=o_, d=d, cn=cn, jj=jj: nc.tensor.matmul(o_[:, cn], lhsT=Sb[d][:], rhs=qin[ri][d][:, cn], start=False, stop=(jj == 7), skip_group_check=True),
                                   r=[('Sb', d), K('qin')], w=[ok_])
                            ds_, dsk = psbank()
                            vmt, vmk = vms[d]
                            sch.op('pe', lambda ds_=ds_, d=d, nn=nn, vmt=vmt: nc.tensor.matmul(ds_[:, 0:128], lhsT=ktl[ri][d][:], rhs=vmt[:, nn, :], start=True, stop=True), r=[K('ktl'), vmk], w=[dsk])
                            sch.op('dve', lambda d=d, ds_=ds_, nn=nn: nc.vector.scalar_tensor_tensor(out=S32[d][:], in0=S32[d][:], scalar=fl[ri][d][:, nn:nn + 1], in1=ds_[:, 0:128], op0=ALU.mult, op1=ALU.add),
                                   r=[('S32', d), K('fl'), dsk], w=[('S32', d)])
                            sch.op('act', lambda d=d: nc.scalar.copy(out=Sb[d][:], in_=S32[d][:]), r=[('S32', d)], w=[('Sb', d)])
                    for d in range(2):
                        o_, ok_ = ops_[d]
                        sch.op('dve', lambda d=d, o_=o_: nc.vector.tensor_tensor(out=oacc[:, tsl[d]], in0=o_[:, 0:128], in1=oacc[:, tsl[d]], op=ALU.add), r=[ok_, 'oacc'], w=['oacc'])
                if si > 0:
                    for d in range(2):
                        sch.dma('sp', [(sho[si - 1, d, h], S32[d][:])], r=[('S32', d)], w=[('sho', si, d, h)], slot=('sst', d))
            if h == 0:
                dump('oacc', oacc[:], 'oacc')
                if stop_after == 'h2':
                    return True
            finalize_head(nc, sch, L, oacc, zs, onv, og, h, sqh, rsh, sl)
        return False


_NC_CACHE = {}


def host_inputs(core, inp, consts):
    f = np.float32
    xT = np.concatenate([inp['x_sample'][core].T, inp['x_prompt'][2 * core].T, inp['x_prompt'][2 * core + 1].T], axis=1)
    cT = np.stack([inp['c'][core], inp['c_ctx']], axis=1)
    m = {
        'xT': np.ascontiguousarray(xT, f),
        'cT': np.ascontiguousarray(cT, f),
        'consts': consts,
        'w_ada': inp['w_ada'],
        'b_adaT': np.ascontiguousarray(inp['b_ada'].reshape(2, 48, 128).transpose(2, 0, 1), f),
        'norm_gT': np.ascontiguousarray(inp['norm_g'].reshape(2, 4, 8, 128).transpose(3, 0, 1, 2), f),
        'gdn_w_in': inp['gdn_w_in'][0],
        'conv_wT': np.ascontiguousarray(inp['gdn_conv_w'][0].reshape(5, 24, 128).transpose(2, 1, 0), f),
        'alog10': np.ascontiguousarray(np.broadcast_to(inp['gdn_a_log'][0].reshape(1, 1, 16), (128, 10, 16)).reshape(128, 160), f),
        'dtb10': np.ascontiguousarray(np.broadcast_to(inp['gdn_dt_bias'][0].reshape(1, 1, 16), (128, 10, 16)).reshape(128, 160), f),
        'gdn_on': np.ascontiguousarray(inp['gdn_onorm_g'][0].reshape(128, 1), f),
        'gdn_w_out': inp['gdn_w_out'][0],
        'hgrn_w_in': inp['hgrn_w_in'][0],
        'lblT': np.ascontiguousarray(inp['hgrn_lb_logits'].reshape(2, 2, 8, 128).transpose(3, 0, 1, 2).reshape(128, 2, 16), f),
        'hgrn_on': np.ascontiguousarray(inp['hgrn_onorm_g'][0].reshape(128, 1), f),
        'hgrn_w_out': inp['hgrn_w_out'][0],
        'mlp_w1': inp['mlp_w1'],
        'mlp_w2': inp['mlp_w2'],
        'sg0': np.ascontiguousarray(inp['state_gdn'][core, 0], f),
        'sh0': np.ascontiguousarray(inp['state_hgrn'][core, 0], f),
    }
    return m


def kernel(**inputs):
    inp = {k: np.asarray(v) for k, v in inputs.items()}
    if 'nc' not in _NC_CACHE:
        _NC_CACHE['nc'] = build()
    nc = _NC_CACHE['nc']
    consts = make_consts()
    in_maps = [host_inputs(core, inp, consts) for core in range(8)]
    res = run_bass_kernel_spmd(nc, in_maps, core_ids=list(range(8)))
    y_prompt = np.empty((16, 256, 1024), np.float32)
    y_sample = np.empty((8, 2048, 1024), np.float32)
    ng = np.empty((16, 1, 2, 8, 128, 128), np.float32)
    nh = np.empty((16, 1, 2, 8, 128, 128), np.float32)
    for core in range(8):
        r = res.results[core]
        yT = r['yT']
        y_sample[core] = yT[:, 0:2048].T
        y_prompt[2 * core] = yT[:, 2048:2304].T
        y_prompt[2 * core + 1] = yT[:, 2304:2560].T
        ng[2 * core:2 * core + 2, 0] = r['sgo']
        nh[2 * core:2 * core + 2, 0] = r['sho']
    return (y_prompt, y_sample, ng, nh)
```

```python
import contextlib
import os
import numpy as np
import concourse.bass as bass
import concourse.mybir as mybir
from concourse.bass_utils import run_bass_kernel_spmd

F32, BF16 = mybir.dt.float32, mybir.dt.bfloat16
AF = mybir.ActivationFunctionType
ALU = mybir.AluOpType

T = 2560
OPSB = int(os.environ.get('OPSB', '6'))
BARR = int(os.environ.get('BARR', '0'))
NCH = 5
NTL = 20
SEQS = [(0, 16), (16, 2), (18, 2)]
EPS = 1e-6
D = 1024

C_ONES, C_ID, C_MF64, C_MB64, C_BLK64, C_SF64, C_SB64, C_MF16, C_MB16, C_SF16, C_SB16, C_CM = [i * 128 for i in range(12)]
NCONST = 12 * 128


_UN = [0]


def _un(name):
    _UN[0] += 1
    return f"sb{_UN[0]}_{name}"


def make_consts():
    t = np.arange(128)
    blk64 = (t[:, None] // 64) == (t[None, :] // 64)
    blk16 = (t[:, None] // 16) == (t[None, :] // 16)
    le = t[:, None] <= t[None, :]
    ge = t[:, None] >= t[None, :]
    lt = t[:, None] < t[None, :]
    gt = t[:, None] > t[None, :]
    cm = np.zeros((128, 128), np.float32)
    cm[t, t // 16] = 1.0
    mats = [np.ones((128, 128)), np.eye(128), blk64 & le, blk64 & ge, blk64, blk64 & lt, blk64 & gt,
            blk16 & le, blk16 & ge, blk16 & lt, blk16 & gt, cm]
    return np.ascontiguousarray(np.concatenate([m.astype(np.float32) for m in mats], axis=1))


class Sched:
    EPOCH = 1 << 30

    def __init__(self, nc, stack):
        self.nc = nc
        self.stack = stack
        self.eng = {'pe': nc.tensor, 'act': nc.scalar, 'dve': nc.vector, 'pool': nc.gpsimd, 'sp': nc.sync}
        self.cur = {}
        self.waited = {}
        self.lw = {}
        self.rd = {}
        self.nsem = 0
        self.dsem = {}
        self.allsems = {}

    def newsem(self):
        self.nsem += 1
        s = self.stack.enter_context(self.nc.semaphore(f"s{self.nsem}"))
        self.allsems[id(s)] = s
        return s

    def _tick(self, e):
        c = self.cur.get(e)
        if c is None or c[1] >= self.EPOCH:
            c = [self.newsem(), 0]
            self.cur[e] = c
        c[1] += 1
        return (c[0], c[1], e)

    def _wait(self, e, tok):
        sem, val, _ = tok
        k = (e, id(sem))
        if self.waited.get(k, 0) >= val:
            return
        self.waited[k] = val
        self.eng[e].wait_ge(sem, val)

    def deps(self, e, r, w):
        toks = []
        for k in r:
            t = self.lw.get(k)
            if t:
                toks.append(t)
        for k in w:
            t = self.lw.get(k)
            if t:
                toks.append(t)
            toks.extend(self.rd.get(k, {}).values())
        for t in toks:
            if e == 'pe' and t[2] == 'pe':
                continue
            self._wait(e, t)

    def commit(self, tok, r, w):
        for k in r:
            d = self.rd.setdefault(k, {})
            o = d.get(id(tok[0]))
            if o is None or o[1] < tok[1]:
                d[id(tok[0])] = tok
        for k in w:
            self.lw[k] = tok
            self.rd[k] = {}

    def op(self, e, fn, r=(), w=()):
        w = list(w) + [k for k in r if isinstance(k, tuple) and k and k[0] == 'ps' and k not in w]
        self.deps(e, r, w)
        inst = fn()
        tok = self._tick(e)
        inst.then_inc(tok[0], 1)
        self.commit(tok, r, w)

    def dma(self, e, pairs, r=(), w=(), slot=None):
        self.deps(e, r, w)
        if isinstance(slot, tuple) and slot[0] == 'dbg':
            slot = 'dbg'
        ds = self.dsem.get(slot)
        if ds is None:
            ds = [self.newsem(), 0]
            self.dsem[slot] = ds
        for out, in_ in pairs:
            inst = self.eng[e].dma_start(out=out, in_=in_)
            ds[1] += 16
            inst.then_inc(ds[0], 16)
        tok = (ds[0], ds[1], 'dma')
        self.commit(tok, r, w)

    def barrier(self):
        toks = []
        for e, c in self.cur.items():
            toks.append((c[0], c[1], e))
        for s, ds in self.dsem.items():
            toks.append((ds[0], ds[1], 'dma'))
        for e in ('pe', 'act', 'dve', 'pool', 'sp'):
            for t in toks:
                if t[1] > 0:
                    self._wait(e, t)

    def final(self):
        for s, ds in self.dsem.items():
            if ds[1] > 0:
                self._wait('sp', (ds[0], ds[1], 'dma'))
        for e, c in self.cur.items():
            if e != 'sp':
                self._wait('sp', (c[0], c[1], e))


def build(stop_after=None, dbg=None):
    nc = bass.Bass("TRN2", target_bir_lowering=False)
    dbg = dbg or []

    def din(name, shape):
        return nc.dram_tensor(name, list(shape), F32, kind="ExternalInput").ap()

    def dout(name, shape):
        return nc.dram_tensor(name, list(shape), F32, kind="ExternalOutput").ap()

    xT = din("xT", [D, T])
    cT = din("cT", [D, 2])
    consts_d = din("consts", [128, NCONST])
    w_ada = din("w_ada", [2, D, 6 * D])
    b_adaT = din("b_adaT", [128, 2, 48])
    norm_gT = din("norm_gT", [128, 2, 4, 8])
    gdn_w_in = din("gdn_w_in", [D, 4128])
    conv_wT = din("conv_wT", [128, 24, 5])
    alog10 = din("alog10", [128, 160])
    dtb10 = din("dtb10", [128, 160])
    gdn_on = din("gdn_on", [128, 1])
    gdn_w_out = din("gdn_w_out", [D, D])
    hgrn_w_in = din("hgrn_w_in", [D, 5120])
    lblT = din("lblT", [128, 2, 16])
    hgrn_on = din("hgrn_on", [128, 1])
    hgrn_w_out = din("hgrn_w_out", [D, D])
    mlp_w1 = din("mlp_w1", [2, D, 4 * D])
    mlp_w2 = din("mlp_w2", [2, 4 * D, D])
    sg0 = din("sg0", [2, 8, 128, 128])
    sh0 = din("sh0", [2, 8, 128, 128])
    yT = dout("yT", [D, T])
    sgo = dout("sgo", [2, 2, 8, 128, 128])
    sho = dout("sho", [2, 2, 8, 128, 128])
    xs = nc.dram_tensor("xs", [D, T], F32, kind="Internal").ap()
    dbg_out = {}
    for name, shape, dt in dbg:
        dbg_out[name] = nc.dram_tensor("dbg_" + name, list(shape), dt, kind="ExternalOutput").ap()

    xT_v = xT.rearrange("(k p) t -> p k t", p=128)
    xs_v = xs.rearrange("(k p) t -> p k t", p=128)
    yT_v = yT.rearrange("(k p) t -> p k t", p=128)

    class Stop(Exception):
        pass

    with contextlib.ExitStack() as stack:
        sch = Sched(nc, stack)
        E = stack.enter_context

        def sb(name, shape, dt):
            return E(nc.sbuf_tensor(_un(name), list(shape), dt))

        consts = sb("consts", [128, NCONST], F32)
        onesb = sb("onesb", [128, 128], BF16)
        identb = sb("identb", [128, 128], BF16)
        cin = sb("cin", [128, 8, 2], F32)
        scT = sb("scT", [128, 8, 2], BF16)
        badat = sb("badat", [128, 2, 48], F32)
        normg = sb("normg", [128, 2, 4, 8], F32)
        mod = sb("mod", [128, 2, 48, 2], F32)
        der = sb("der", [128, 2, 4, 8, 2], F32)
        hA = sb("hA", [128, 8, T], BF16)
        slabs = [sb(f"slab{i}", [128, 8192], BF16) for i in range(2)]
        psb_ = [E(nc.psum_tensor(f"ps{i}", [128, 512], F32)) for i in range(8)]
        pst = None

        def cst(off, n=128):
            return consts[:, off:off + n]

        st = {'pb': 0, 'slab': 0}

        def psbank():
            b = st['pb'] % int(os.environ.get('NRING', '6'))
            st['pb'] += 1
            return psb_[b], ('ps', b)

        def psq():
            t_, k_ = psbank()
            return t_[:, 0:128], k_

        def psh():
            t_, k_ = psbank()
            return t_[:, 0:256], [k_]

        def next_slab():
            i = st['slab'] % 2
            st['slab'] += 1
            return slabs[i], ('slab', i)

        def mm_group(out, pairs):
            def fn():
                n = len(pairs)
                inst = None
                for i, (l, r_) in enumerate(pairs):
                    inst = nc.tensor.matmul(out, lhsT=l, rhs=r_, start=(i == 0), stop=(i == n - 1))
                return inst
            return fn

        def dump(name, ap, key):
            if name in dbg_out:
                sch.dma('sp', [(dbg_out[name], ap)], r=[key], w=[('dbgout', name)], slot=('dbg', name))

        def program():
            sch.dma('sp', [(consts[:], consts_d), (cin[:], cT.rearrange("(k p) c -> p k c", p=128)), (badat[:], b_adaT), (normg[:], norm_gT)],
                    w=['consts', 'cin', 'badat', 'normg'], slot='setup')
            sch.op('dve', lambda: nc.vector.tensor_copy(out=onesb[:], in_=cst(C_ONES)), r=['consts'], w=['onesb'])
            sch.op('dve', lambda: nc.vector.tensor_copy(out=identb[:], in_=cst(C_ID)), r=['consts'], w=['identb'])
            sch.op('act', lambda: nc.scalar.activation(out=scT[:], in_=cin[:], func=AF.Silu), r=['cin'], w=['scT'])

            for l in range(2):
                mps, mkey = psbank()
                for s in range(6):
                    slab, skey = next_slab()
                    sv = slab[:, 0:8192].rearrange("p (k n) -> p k n", k=8)
                    sch.dma('pool', [(sv, w_ada[l, :, s * 1024:(s + 1) * 1024].rearrange("(k p) n -> p k n", p=128))],
                            w=[skey], slot=skey)
                    for j in range(8):
                        col = (s * 8 + j) * 2
                        sch.op('pe', mm_group(mps[:, col:col + 2],
                                              [(sv[:, k, j * 128:(j + 1) * 128], scT[:, k, :]) for k in range(8)]),
                               r=[skey, 'scT'], w=[mkey])
                mv = mps[:, 0:96].rearrange("p (j c) -> p j c", c=2)
                for c in range(2):
                    sch.op('dve', lambda c=c: nc.vector.tensor_tensor(out=mod[:, l, :, c], in0=mv[:, :, c], in1=badat[:, l, :], op=ALU.add),
                           r=[mkey, 'badat'], w=['mod'])
                for c in range(2):
                    sch.op('dve', lambda c=c: nc.vector.scalar_tensor_tensor(out=der[:, l, 0, :, c], in0=mod[:, l, 8:16, c], scalar=1.0, in1=normg[:, l, 0, :], op0=ALU.add, op1=ALU.mult), r=['mod', 'normg'], w=['der'])
                    sch.op('dve', lambda c=c: nc.vector.tensor_tensor(out=der[:, l, 1, :, c], in0=mod[:, l, 16:24, c], in1=normg[:, l, 1, :], op=ALU.mult), r=['mod', 'normg'], w=['der'])
                    sch.op('dve', lambda c=c: nc.vector.scalar_tensor_tensor(out=der[:, l, 2, :, c], in0=mod[:, l, 32:40, c], scalar=1.0, in1=normg[:, l, 2, :], op0=ALU.add, op1=ALU.mult), r=['mod', 'normg'], w=['der'])
                    sch.op('dve', lambda c=c: nc.vector.tensor_tensor(out=der[:, l, 3, :, c], in0=mod[:, l, 40:48, c], in1=normg[:, l, 3, :], op=ALU.mult), r=['mod', 'normg'], w=['der'])
            dump('mod', mod[:], 'mod')
            dump('der', der[:], 'der')
            if stop_after == 'ada':
                return

            def norm_to_hA(xbuf, xkey, c, l, gi, shoff, sq8, rs, t1):
                cond = 0 if c < 4 else 1
                sch.op('act', lambda: nc.scalar.activation(out=sq8[:], in_=xbuf, func=AF.Square), r=[xkey], w=['sq8'])
                ss, sskey = psbank()
                sch.op('pe', mm_group(ss[:, :], [(onesb[:], sq8[:, k, :]) for k in range(8)]), r=['sq8', 'onesb'], w=[sskey])
                sch.op('act', lambda: nc.scalar.activation(out=rs[:], in_=ss[:, :], func=AF.Sqrt, bias=EPS, scale=1.0 / D), r=[sskey], w=['rs'])
                sch.op('dve', lambda: nc.vector.reciprocal(out=rs[:], in_=rs[:]), r=['rs'], w=['rs'])
                for k in range(8):
                    tk = ('t1', k % 2)
                    sch.op('dve', lambda k=k: nc.vector.tensor_tensor(out=t1[:, k % 2, :], in0=xbuf[:, k, :], in1=rs[:], op=ALU.mult), r=[xkey, 'rs'], w=[tk])
                    sch.op('act', lambda k=k: nc.scalar.activation(out=hA[:, k, c * 512:(c + 1) * 512], in_=t1[:, k % 2, :], func=AF.Identity,
                                                                  scale=der[:, l, gi, k, cond:cond + 1], bias=mod[:, l, shoff + k, cond:cond + 1]),
                           r=[tk, 'der', 'mod'], w=[('hA', c)])

            def epilogue(src, srckey, c, l, ggi, x_src, x_src_key, x_dst, x_dst_key, nxt, xin, sq8, rs, t1):
                cond = 0 if c < 4 else 1
                xb = xin[c % len(xin)]
                xkey = ('xin', c % len(xin))
                sch.dma('sp', [(xb[:], x_src[:, :, c * 512:(c + 1) * 512])], r=[(x_src_key, c)], w=[xkey], slot=xkey)
                sch.op('act', lambda: nc.scalar.activation(out=sq8[:], in_=src, func=AF.Square), r=[srckey], w=['sq8'])
                ss, sskey = psbank()
                sch.op('pe', mm_group(ss[:, :], [(onesb[:], sq8[:, k, :]) for k in range(8)]), r=['sq8', 'onesb'], w=[sskey])
                sch.op('act', lambda: nc.scalar.activation(out=rs[:], in_=ss[:, :], func=AF.Sqrt, bias=EPS, scale=1.0 / D), r=[sskey], w=['rs'])
                sch.op('dve', lambda: nc.vector.reciprocal(out=rs[:], in_=rs[:]), r=['rs'], w=['rs'])
                for k in range(8):
                    tk = ('t1', k % 2)
                    sch.op('dve', lambda k=k: nc.vector.tensor_tensor(out=t1[:, k % 2, :], in0=src[:, k, :], in1=rs[:], op=ALU.mult), r=[srckey, 'rs'], w=[tk])
                    sch.op('dve', lambda k=k: nc.vector.scalar_tensor_tensor(out=xb[:, k, :], in0=t1[:, k % 2, :], scalar=der[:, l, ggi, k, cond:cond + 1], in1=xb[:, k, :], op0=ALU.mult, op1=ALU.add),
                           r=[tk, 'der', xkey], w=[xkey])
                sch.dma('sp', [(x_dst[:, :, c * 512:(c + 1) * 512], xb[:])], r=[xkey], w=[(x_dst_key, c)], slot=('xst', c % len(xin)))
                if nxt is not None:
                    nl, gi, shoff = nxt
                    norm_to_hA(xb[:], xkey, c, nl, gi, shoff, sq8, rs, t1)

            def dense_fm(slab_view_fn, skey, ct_list, consume):
                for ct in ct_list:
                    for c in range(NCH):
                        ps, pkey = psbank()
                        sch.op('pe', mm_group(ps[:, :], [(slab_view_fn(ct, k), hA[:, k, c * 512:(c + 1) * 512]) for k in range(8)]),
                               r=[skey, ('hA', c)], w=[pkey])
                        consume(ct, c, ps, pkey)

            for l in range(2):
                if l == 0:
                    with contextlib.ExitStack() as es:
                        xin = [es.enter_context(nc.sbuf_tensor(_un(f"xin{i}"), [128, 8, 512], F32)) for i in range(2)]
                        sq8 = es.enter_context(nc.sbuf_tensor(_un("sq8"), [128, 8, 512], BF16))
                        rs = es.enter_context(nc.sbuf_tensor(_un("rs"), [128, 512], F32))
                        t1 = es.enter_context(nc.sbuf_tensor(_un("t1"), [128, 2, 512], F32))
                        for c in range(NCH):
                            xb = xin[c % 2]
                            xkey = ('xin', c % 2)
                            sch.dma('sp', [(xb[:], xT_v[:, :, c * 512:(c + 1) * 512])], w=[xkey], slot=xkey)
                            norm_to_hA(xb[:], xkey, c, 0, 0, 0, sq8, rs, t1)
                        sch.barrier()
                    dump('hA0', hA[:], ('hA', 0))
                    if stop_after == 'n1':
                        return

                x_src = xT_v if l == 0 else xs_v
                x_src_key = 'xT' if l == 0 else 'xs'

                with contextlib.ExitStack() as es_og:
                    og = es_og.enter_context(nc.sbuf_tensor(_un("og"), [128, 8, T], BF16))
                    stopped = (gdn_mixer if l == 0 else hgrn_mixer)(nc, sch, dict(CTX, og=og))
                    if stopped:
                        return
                    sch.barrier()
                    if stop_after == f'mix{l}':
                        dump('og', og[:], ('og', 0))
                        return

                    with contextlib.ExitStack() as es:
                        xin = [es.enter_context(nc.sbuf_tensor(_un(f"xin{i}"), [128, 8, 512], F32)) for i in range(2)]
                        sq8 = es.enter_context(nc.sbuf_tensor(_un("sq8"), [128, 8, 512], BF16))
                        rs = es.enter_context(nc.sbuf_tensor(_un("rs"), [128, 512], F32))
                        t1 = es.enter_context(nc.sbuf_tensor(_un("t1"), [128, 2, 512], F32))
                        srcb = es.enter_context(nc.sbuf_tensor(_un("srcb"), [128, 8, 512], F32))
                        slab, skey = next_slab()
                        sv = slab[:, 0:8192].rearrange("p (k n) -> p k n", k=8)
                        wout = gdn_w_out if l == 0 else hgrn_w_out
                        sch.dma('pool', [(sv, wout.rearrange("(k p) n -> p k n", p=128))], w=[skey], slot=skey)
                        for c in range(NCH):
                            for m in range(8):
                                ps, pkey = psbank()
                                sch.op('pe', mm_group(ps[:, :], [(sv[:, k, m * 128:(m + 1) * 128], og[:, k, c * 512:(c + 1) * 512]) for k in range(8)]),
                                       r=[skey, ('og', c)], w=[pkey])
                                sch.op('act', lambda m=m, ps=ps: nc.scalar.copy(out=srcb[:, m, :], in_=ps[:, :]), r=[pkey], w=['srcb'])
                            epilogue(srcb[:], 'srcb', c, l, 1, x_src, x_src_key, xs_v, 'xs', (l, 2, 24), xin, sq8, rs, t1)
                        sch.barrier()
                if stop_after == f'epi1_{l}':
                    dump('hA0', hA[:], ('hA', 0))
                    return

                with contextlib.ExitStack() as es:
                    ffacc = es.enter_context(nc.sbuf_tensor(_un("ffacc"), [128, 8, T], F32))
                    hid = es.enter_context(nc.sbuf_tensor(_un("hid"), [128, 2, 4, 512], BF16))
                    hr = es.enter_context(nc.sbuf_tensor(_un("hr"), [128, 2, 512], F32))
                    xin = [es.enter_context(nc.sbuf_tensor(_un(f"xin{i}"), [128, 8, 512], F32)) for i in range(1)]
                    sq8 = es.enter_context(nc.sbuf_tensor(_un("sq8"), [128, 8, 512], BF16))
                    rs = es.enter_context(nc.sbuf_tensor(_un("rs"), [128, 512], F32))
                    t1 = es.enter_context(nc.sbuf_tensor(_un("t1"), [128, 2, 512], F32))
                    nhr = 0
                    for g in range(8):
                        slab, skey = next_slab()
                        w1v = slab[:, 0:4096].rearrange("p (k n) -> p k n", k=8)
                        w2v = slab[:, 4096:8192].rearrange("p (j n) -> p j n", j=4)
                        sch.dma('pool', [(w1v, mlp_w1[l, :, g * 512:(g + 1) * 512].rearrange("(k p) n -> p k n", p=128)),
                                         (w2v, mlp_w2[l, g * 512:(g + 1) * 512, :].rearrange("(j p) n -> p j n", p=128))],
                                w=[skey], slot=skey)
                        for c in range(NCH):
                            hb = (g * NCH + c) % 2
                            for j in range(4):
                                ps, pkey = psbank()
                                sch.op('pe', mm_group(ps[:, :], [(w1v[:, k, j * 128:(j + 1) * 128], hA[:, k, c * 512:(c + 1) * 512]) for k in range(8)]),
                                       r=[skey, ('hA', c)], w=[pkey])
                                hk = ('hr', nhr % 2)
                                sch.op('act', lambda ps=ps, i=nhr % 2: nc.scalar.activation(out=hr[:, i, :], in_=ps[:, :], func=AF.Relu), r=[pkey], w=[hk])
                                sch.op('dve', lambda i=nhr % 2, hb=hb, j=j: nc.vector.tensor_tensor(out=hid[:, hb, j, :], in0=hr[:, i, :], in1=hr[:, i, :], op=ALU.mult),
                                       r=[hk], w=[('hid', hb)])
                                nhr += 1
                            for m in range(8):
                                ps, pkey = psbank()
                                sch.op('pe', mm_group(ps[:, :], [(w2v[:, j, m * 128:(m + 1) * 128], hid[:, hb, j, :]) for j in range(4)]),
                                       r=[skey, ('hid', hb)], w=[pkey])
                                if g == 0:
                                    sch.op('act', lambda ps=ps, m=m, c=c: nc.scalar.copy(out=ffacc[:, m, c * 512:(c + 1) * 512], in_=ps[:, :]), r=[pkey], w=[('ff', c)])
                                else:
                                    sch.op('dve', lambda ps=ps, m=m, c=c: nc.vector.tensor_tensor(out=ffacc[:, m, c * 512:(c + 1) * 512], in0=ps[:, :], in1=ffacc[:, m, c * 512:(c + 1) * 512], op=ALU.add),
                                           r=[pkey, ('ff', c)], w=[('ff', c)])
                    for c in range(NCH):
                        if l == 0:
                            epilogue(ffacc[:, :, c * 512:(c + 1) * 512], ('ff', c), c, l, 3, xs_v, 'xs', xs_v, 'xs', (1, 0, 0), xin, sq8, rs, t1)
                        else:
                            epilogue(ffacc[:, :, c * 512:(c + 1) * 512], ('ff', c), c, l, 3, xs_v, 'xs', yT_v, 'yT', None, xin, sq8, rs, t1)
                    sch.barrier()
                if stop_after == f'layer{l}':
                    dump('hA0', hA[:], ('hA', 0))
                    return
        CTX = dict(hA=hA, consts=consts, onesb=onesb, identb=identb, psbank=psbank, psq=psq, psh=psh, next_slab=next_slab,
                   mm_group=mm_group, cst=cst, dump=dump, pst=pst, psb_=psb_, gdn_w_in=gdn_w_in, conv_wT=conv_wT, alog10=alog10,
                   dtb10=dtb10, gdn_on=gdn_on, sg0=sg0, sgo=sgo, hgrn_w_in=hgrn_w_in, lblT=lblT, hgrn_on=hgrn_on, sh0=sh0, sho=sho,
                   stop_after=stop_after)
        program()
        sch.final()
    return nc


def gdn_mixer(nc, sch, L):
    hA, og, consts, onesb, identb = L['hA'], L['og'], L['consts'], L['onesb'], L['identb']
    psbank, psq, psh, next_slab, mm_group, cst = L['psbank'], L['psq'], L['psh'], L['next_slab'], L['mm_group'], L['cst']
    dump, pst = L['dump'], L['pst']
    gdn_w_in, conv_wT, alog10, dtb10, gdn_on, sg0, sgo = L['gdn_w_in'], L['conv_wT'], L['alog10'], L['dtb10'], L['gdn_on'], L['sg0'], L['sgo']
    stop_after = L['stop_after']
    SCALE = 128 ** -0.5
    with contextlib.ExitStack() as es:
        def sb(name, shape, dt):
            return es.enter_context(nc.sbuf_tensor(_un(name), list(shape), dt))
        wbg = sb("wbg", [128, 8, 32], BF16)
        cw = sb("cw", [128, 24, 5], F32)
        al = sb("al", [128, 160], F32)
        dtb = sb("dtb", [128, 160], F32)
        onv = sb("onv", [128, 1], F32)
        beta = sb("beta", [128, NTL, 16], F32)
        gg = sb("gg", [128, NTL, 16], F32)
        gc = sb("gc", [128, NTL, 16], F32)
        kts = sb("kts", [128, NTL, 16], F32)
        tA = sb("tA", [128, 160], F32)
        preS = [sb(f"preS{i}", [128, 544], F32) for i in range(2)]
        preP = sb("preP", [128, 520], F32)
        cv = sb("cv", [128, 2, 512], F32)
        sl = sb("sl", [128, 512], F32)
        sqh = sb("sqh", [128, 512], BF16)
        rsh = sb("rsh", [128, 512], F32)
        qT = sb("qT", [128, T], BF16)
        kT = sb("kT", [128, T], BF16)
        vT = sb("vT", [128, T], BF16)
        ktok = sb("ktok", [128, NTL, 128], BF16)
        vtok = sb("vtok", [128, NTL, 128], BF16)
        zs = sb("zs", [128, T], BF16)
        oacc = sb("oacc", [128, T], F32)
        R2 = int(os.environ.get('R2', '2'))
        R1 = 1
        eGb = [sb(f"eGb{i}", [128, 2, 128], F32) for i in range(R2)]
        Dm = [sb(f"Dm{i}", [128, 2, 128], F32) for i in range(R1)]
        MFg = Dm
        dec = Dm
        decI = [sb(f"decI{i}", [128, 2, 128], F32) for i in range(R1)]
        decS = [sb(f"decS{i}", [128, 2, 128], F32) for i in range(R1)]
        Pm = [[sb(f"Pm{i}_{j}", [128, 2, 128], F32) for j in range(2)] for i in range(R1)]
        PTm = [[sb(f"PTm{i}_{j}", [128, 2, 128], F32) for j in range(2)] for i in range(R1)]
        Xm = [[sb(f"Xm{i}_{j}", [128, 2, 128], F32) for j in range(2)] for i in range(R1)]
        Xbf = [sb(f"Xbf{i}", [128, 2, 128], BF16) for i in range(R2)]
        qkT = [sb(f"qkT{i}", [128, 2, 128], BF16) for i in range(R2)]
        kgT = [sb(f"kgT{i}", [128, 2, 128], BF16) for i in range(R2)]
        qdT = [sb(f"qdT{i}", [128, 2, 128], BF16) for i in range(R2)]
        ktl = [sb(f"ktl{i}", [128, 2, 128], BF16) for i in range(R2)]
        S32 = [sb(f"S32_{d}", [128, 128], F32) for d in range(2)]
        Sb = [sb(f"Sb_{d}", [128, 128], BF16) for d in range(2)]
        vnew = [[sb(f"vnew{d}_{j}", [128, 128], BF16) for j in range(2)] for d in range(2)]
        rr = [sb(f"rr{d}", [128, 128], BF16) for d in range(2)]
        nbeta = sb("nbeta", [128, NTL, 16], F32)
        MFgb = sb("MFgb", [128, 2, 128], BF16)

        sch.dma('pool', [(wbg[:], gdn_w_in[:, 4096:4128].rearrange("(k p) n -> p k n", p=128))], w=['wbg'], slot='wbg')
        if os.environ.get('GS', '0') == '1':
            sch.dma('sp', [(cw[:], conv_wT), (al[:], alog10), (dtb[:], dtb10), (onv[:], gdn_on)], w=['cw', 'al', 'dtb', 'onv'], slot='gsetup')
        else:
            sch.dma('sp', [(cw[:], conv_wT)], w=['cw'], slot='gs0')
            sch.dma('sp', [(al[:], alog10)], w=['al'], slot='gs1')
            sch.dma('sp', [(dtb[:], dtb10)], w=['dtb'], slot='gs2')
            sch.dma('sp', [(onv[:], gdn_on)], w=['onv'], slot='gs3')
        sch.op('act', lambda: nc.scalar.activation(out=al[:], in_=al[:], func=AF.Exp), r=['al'], w=['al'])
        sch.op('dve', lambda: nc.vector.tensor_scalar(out=al[:], in0=al[:], scalar1=-1.0, scalar2=None, op0=ALU.mult), r=['al'], w=['al'])
        for d in range(2):
            for j in range(2):
                sch.op('dve', lambda d=d, j=j: nc.vector.memset(vnew[d][j][:], 0.0), w=[('vnew', d, j)])
            sch.op('dve', lambda d=d: nc.vector.memset(rr[d][:], 0.0), w=[('rr', d)])
        for i in range(2):
            sch.op('dve', lambda i=i: nc.vector.memset(preS[i][:], 0.0), w=[('preS', i)])
        sch.op('dve', lambda: nc.vector.memset(preP[:], 0.0), w=['preP'])

        for b in range(2):
            ps, pkey = psbank()
            pv = ps[:, 0:320].rearrange("p (i n) -> p i n", n=32)
            for ii in range(10):
                i = b * 10 + ii
                sch.op('pe', mm_group(pv[:, ii, :], [(hA[:, k, i * 128:(i + 1) * 128], wbg[:, k, :]) for k in range(8)]),
                       r=[('hA', i // 4), 'wbg'], w=[pkey])
            sch.op('act', lambda pv=pv, b=b: nc.scalar.activation(out=beta[:, b * 10:(b + 1) * 10, :], in_=pv[:, :, 0:16], func=AF.Sigmoid), r=[pkey], w=['beta'])
            tAv = tA[:, :].rearrange("p (i n) -> p i n", n=16)
            sch.op('dve', lambda pv=pv: nc.vector.tensor_tensor(out=tAv, in0=pv[:, :, 16:32], in1=dtb[:, :].rearrange("p (i n) -> p i n", n=16), op=ALU.add), r=[pkey, 'dtb'], w=['tA'])
            sch.op('act', lambda: nc.scalar.activation(out=tA[:], in_=tA[:], func=AF.Exp), r=['tA'], w=['tA'])
            sch.op('act', lambda: nc.scalar.activation(out=tA[:], in_=tA[:], func=AF.Ln, bias=1.0, scale=1.0), r=['tA'], w=['tA'])
            sch.op('dve', lambda b=b: nc.vector.tensor_tensor(out=gg[:, b * 10:(b + 1) * 10, :], in0=tAv, in1=al[:, :].rearrange("p (i n) -> p i n", n=16), op=ALU.mult), r=['tA', 'al'], w=['gg'])
        sch.op('dve', lambda: nc.vector.tensor_scalar(out=nbeta[:], in0=beta[:], scalar1=-1.0, scalar2=None, op0=ALU.mult), r=['beta'], w=['nbeta'])
        for b in range(2):
            ps, pkey = psbank()
            pv = ps[:, 0:320].rearrange("p (i n) -> p i n", n=32)
            for ii in range(10):
                i = b * 10 + ii
                sch.op('pe', lambda i=i, ii=ii, pv=pv: nc.tensor.matmul(pv[:, ii, 0:8], lhsT=cst(C_MF64), rhs=gg[:, i, 0:8], start=True, stop=True), r=['gg', 'consts'], w=[pkey])
                sch.op('pe', lambda i=i, ii=ii, pv=pv: nc.tensor.matmul(pv[:, ii, 8:16], lhsT=cst(C_MB64), rhs=gg[:, i, 8:16], start=True, stop=True), r=['gg', 'consts'], w=[pkey])
                sch.op('pe', lambda i=i, ii=ii, pv=pv: nc.tensor.matmul(pv[:, ii, 16:32], lhsT=cst(C_BLK64), rhs=gg[:, i, 0:16], start=True, stop=True), r=['gg', 'consts'], w=[pkey])
            sch.op('act', lambda pv=pv, b=b: nc.scalar.copy(out=gc[:, b * 10:(b + 1) * 10, :], in_=pv[:, :, 0:16]), r=[pkey], w=['gc'])
            sch.op('dve', lambda pv=pv, b=b: nc.vector.tensor_tensor(out=kts[:, b * 10:(b + 1) * 10, :], in0=pv[:, :, 16:32], in1=gc[:, b * 10:(b + 1) * 10, :], op=ALU.subtract), r=[pkey, 'gc'], w=['kts'])
        sch.op('act', lambda: nc.scalar.activation(out=kts[:], in_=kts[:], func=AF.Exp), r=['kts'], w=['kts'])
        dump('beta', beta[:], 'beta')
        dump('gg', gg[:], 'gg')
        dump('gc', gc[:], 'gc')
        dump('kts', kts[:], 'kts')
        if stop_after == 'g0':
            return True

        nq = [0]
        for h in range(8):
            slab, skey = next_slab()
            sv = slab[:, 0:4096].rearrange("p (c k n) -> p c k n", c=4, k=8)
            sch.dma('pool', [(sv[:, ci], gdn_w_in[:, off:off + 128].rearrange("(k p) n -> p k n", p=128))
                             for ci, off in enumerate([h * 128, 1024 + h * 128, 2048 + h * 128, 3072 + h * 128])],
                    w=[skey], slot=skey)
            for ct in [int(x) for x in os.environ.get('CTS', '0,1,2,3').split(',')]:
                for c in range(NCH):
                    ps, pkey = psbank()
                    sch.op('pe', mm_group(ps[:, :], [(sv[:, ct, k, :], hA[:, k, c * 512:(c + 1) * 512]) for k in range(8)]),
                           r=[skey, ('hA', c)], w=[pkey])
                    csl = slice(c * 512, (c + 1) * 512)
                    if ct == 3:
                        sch.op('act', lambda ps=ps, csl=csl: nc.scalar.activation(out=zs[:, csl], in_=ps[:, :], func=AF.Silu), r=[pkey], w=['zs'])
                        continue
                    i2 = nq[0] % 2
                    nq[0] += 1
                    if c < 4:
                        pre, prek = preS[i2], ('preS', i2)
                        p3 = pre[:, 0:544].rearrange("p (r n) -> p r n", n=68)
                        W = 64
                        psv = ps[:, :].rearrange("p (r n) -> p r n", n=64)
                    else:
                        pre, prek = preP, 'preP'
                        p3 = pre[:, 0:520].rearrange("p (r n) -> p r n", n=260)
                        W = 256
                        psv = ps[:, :].rearrange("p (r n) -> p r n", n=256)
                    sch.op('act', lambda p3=p3, psv=psv, W=W: nc.scalar.copy(out=p3[:, :, 2:2 + W], in_=psv), r=[pkey], w=[prek])
                    cvv = cv[:, i2, :].rearrange("p (r n) -> p r n", n=W)
                    cvk = ('cv', i2)
                    wt = ct * 8 + h
                    sch.op('dve', lambda p3=p3, cvv=cvv, W=W, wt=wt: nc.vector.tensor_scalar(out=cvv, in0=p3[:, :, 0:W], scalar1=cw[:, wt, 0:1], scalar2=None, op0=ALU.mult), r=[prek, 'cw'], w=[cvk])
                    for j in range(1, 5):
                        sch.op('dve', lambda p3=p3, cvv=cvv, W=W, wt=wt, j=j: nc.vector.scalar_tensor_tensor(out=cvv, in0=p3[:, :, j:j + W], scalar=cw[:, wt, j:j + 1], in1=cvv, op0=ALU.mult, op1=ALU.add), r=[prek, 'cw', cvk], w=[cvk])
                    if ct == 2:
                        sch.op('act', lambda i2=i2, csl=csl: nc.scalar.activation(out=vT[:, csl], in_=cv[:, i2, :], func=AF.Silu), r=[cvk], w=['vT'])
                        continue
                    slk = 'sl'
                    sch.op('act', lambda i2=i2: nc.scalar.activation(out=sl[:, :], in_=cv[:, i2, :], func=AF.Silu), r=[cvk], w=[slk])
                    NOL2 = int(os.environ.get('NOL2', '0'))
                    if NOL2 == 1:
                        continue
                    sch.op('act', lambda i2=i2: nc.scalar.activation(out=sqh[:, :], in_=sl[:, :], func=AF.Square), r=[slk], w=['sqh'])
                    if NOL2 == 2:
                        continue
                    ss, sskey = psbank()
                    sch.op('pe', lambda ss=ss, i2=i2: nc.tensor.matmul(ss[:, :], lhsT=onesb[:], rhs=sqh[:, :], start=True, stop=True), r=['sqh', 'onesb'], w=[sskey])
                    if NOL2 == 3:
                        continue
                    sch.op('act', lambda ss=ss, i2=i2: nc.scalar.activation(out=rsh[:, :], in_=ss[:, :], func=AF.Sqrt, bias=EPS, scale=1.0), r=[sskey], w=['rsh'])
                    if NOL2 == 4:
                        continue
                    sch.op('dve', lambda i2=i2: nc.vector.reciprocal(out=rsh[:, :], in_=rsh[:, :]), r=['rsh'], w=['rsh'])
                    dst = qT if ct == 0 else kT
                    dk = 'qT' if ct == 0 else 'kT'
                    sch.op('dve', lambda i2=i2, dst=dst, csl=csl: nc.vector.tensor_tensor(out=dst[:, csl], in0=sl[:, :], in1=rsh[:, :], op=ALU.mult), r=[slk, 'rsh'], w=[dk])
            if stop_after == 'g1a':
                dump('qT', qT[:], 'qT')
                dump('kT', kT[:], 'kT')
                dump('vT', vT[:], 'vT')
                dump('zs', zs[:], 'zs')
                return True
            for src, sk, dstt, dk in ((kT, 'kT', ktok, 'ktok'), (vT, 'vT', vtok, 'vtok')):
                for i4 in range(int(os.environ.get('NTR', '5'))):
                    tb, tbk = psbank()
                    tv = tb[:, 0:512].rearrange("p (i n) -> p i n", n=128)
                    for ii in range(4):
                        i = i4 * 4 + ii
                        sch.op('pe', lambda src=src, i=i, ii=ii, tv=tv: nc.tensor.matmul(tv[:, ii, :], lhsT=src[:, i * 128:(i + 1) * 128], rhs=identb[:], start=True, stop=True), r=[sk, 'identb'], w=[tbk])
                    sch.op('act', lambda dstt=dstt, i4=i4, tv=tv: nc.scalar.copy(out=dstt[:, i4 * 4:(i4 + 1) * 4, :], in_=tv), r=[tbk], w=[dk])
            if h == 0:
                dump('qT', qT[:], 'qT')
                dump('kT', kT[:], 'kT')
                dump('vT', vT[:], 'vT')
                dump('ktok', ktok[:], 'ktok')
                dump('zs', zs[:], 'zs')
                if stop_after == 'g1':
                    return True
            sch.op('dve', lambda: nc.vector.memset(oacc[:], 0.0), w=['oacc'])

            npair = 0
            for si, (t0, n) in enumerate(SEQS):
                if si == 0:
                    for d in range(2):
                        sch.dma('sp', [(S32[d][:], sg0[d, h])], w=[('S32', d)], slot=('S32', d))
                for d in range(2):
                    if si != 0:
                        sch.op('dve', lambda d=d: nc.vector.memset(S32[d][:], 0.0), w=[('S32', d)])
                    sch.op('act', lambda d=d: nc.scalar.copy(out=Sb[d][:], in_=S32[d][:]), r=[('S32', d)], w=[('Sb', d)])
                for p in range(n):
                    ri = npair % R2
                    npair += 1
                    tiles = [t0 + p, t0 + n - 1 - p]
                    cols = [h, 8 + h]
                    tsl = [slice(i * 128, (i + 1) * 128) for i in tiles]
                    K = lambda name: (name, ri)
                    rj = 0
                    KJ = lambda name: (name, rj)
                    _cutn = int(os.environ.get('CUT', '99'))
                    _cnt = [0]

                    def bop(*a, **k):
                        if p >= 1:
                            _cnt[0] += 1
                            if _cnt[0] > _cutn:
                                return
                        sch.op(*a, **k)
                    GBBF = os.environ.get('GBBF', '0') == '1'
                    for d in range(2):
                        bop('dve', lambda d=d: nc.vector.tensor_scalar(out=(MFgb if GBBF else MFg[rj])[:, d, :], in0=cst(C_MF64 if d == 0 else C_MB64), scalar1=gg[:, tiles[d], cols[d]:cols[d] + 1], scalar2=None, op0=ALU.mult),
                               r=['consts', 'gg'], w=[KJ('Dm'), 'MFgb'])
                    gb, gbk = psh()
                    for d in range(2):
                        if GBBF:
                            bop('pe', lambda d=d, gb=gb: nc.tensor.matmul(gb[:, d * 128:(d + 1) * 128], lhsT=onesb[:], rhs=MFgb[:, d, :], start=True, stop=True), r=['onesb', 'MFgb'], w=gbk)
                        else:
                            bop('pe', lambda d=d, gb=gb: nc.tensor.matmul(gb[:, d * 128:(d + 1) * 128], lhsT=cst(C_ONES), rhs=MFg[rj][:, d, :], start=True, stop=True), r=['consts', KJ('Dm')], w=gbk)
                    bop('act', lambda gb=gb: nc.scalar.activation(out=eGb[ri][:, :, :].rearrange("p a b -> p (a b)"), in_=gb, func=AF.Exp), r=gbk, w=[K('eGb')])
                    for d in range(2):
                        bop('dve', lambda d=d, gb=gb: nc.vector.tensor_scalar(out=Dm[rj][:, d, :], in0=gb[:, d * 128:(d + 1) * 128], scalar1=gc[:, tiles[d], cols[d]:cols[d] + 1], scalar2=0.0, op0=ALU.subtract, op1=ALU.min),
                               r=gbk + ['gc', K('eGb')], w=[KJ('Dm')])
                    bop('act', lambda: nc.scalar.activation(out=dec[rj][:], in_=Dm[rj][:], func=AF.Exp), r=[KJ('Dm')], w=[KJ('Dm')])
                    for d in range(2):
                        bop('dve', lambda d=d: nc.vector.tensor_tensor(out=decI[rj][:, d, :], in0=dec[rj][:, d, :], in1=cst(C_MF64 if d == 0 else C_MB64), op=ALU.mult), r=[KJ('Dm'), 'consts'], w=[KJ('decI')])
                        bop('dve', lambda d=d: nc.vector.tensor_tensor(out=decS[rj][:, d, :], in0=dec[rj][:, d, :], in1=cst(C_SF64 if d == 0 else C_SB64), op=ALU.mult), r=[KJ('Dm'), 'consts'], w=[KJ('decS')])
                    SS = int(os.environ.get('SCAN_STOP', '0'))
                    if SS == 1 and p == int(os.environ.get('SSP', '0')):
                        dump('oacc', oacc[:], 'oacc')
                        return True
                    kk, kkk = psh()
                    qk, qkk = psh()
                    for d in range(2):
                        sch.op('pe', lambda d=d, kk=kk: nc.tensor.matmul(kk[:, d * 128:(d + 1) * 128], lhsT=kT[:, tsl[d]], rhs=kT[:, tsl[d]], start=True, stop=True), r=['kT'], w=kkk)
                        sch.op('pe', lambda d=d, qk=qk: nc.tensor.matmul(qk[:, d * 128:(d + 1) * 128], lhsT=kT[:, tsl[d]], rhs=qT[:, tsl[d]], start=True, stop=True), r=['kT', 'qT'], w=qkk)
                    P0, PT0, X0 = Pm[rj][0], PTm[rj][0], Xm[rj][0]
                    for d in range(2):
                        sch.op('dve', lambda d=d, kk=kk: nc.vector.scalar_tensor_tensor(out=P0[:, d, :], in0=kk[:, d * 128:(d + 1) * 128], scalar=nbeta[:, tiles[d], cols[d]:cols[d] + 1], in1=decS[rj][:, d, :], op0=ALU.mult, op1=ALU.mult),
                               r=kkk + ['nbeta', KJ('decS')], w=[KJ('P0')])
                        sch.op('dve', lambda d=d, qk=qk: nc.vector.scalar_tensor_tensor(out=qkT[ri][:, d, :], in0=qk[:, d * 128:(d + 1) * 128], scalar=SCALE, in1=decI[rj][:, d, :], op0=ALU.mult, op1=ALU.mult),
                               r=qkk + [KJ('decI')], w=[K('qkT')])
                    tb, tbk = psh()
                    tv = tb.rearrange("p (i n) -> p i n", n=128)
                    for d in range(2):
                        sch.op('pe', lambda d=d, tv=tv: nc.tensor.matmul(tv[:, d, :], lhsT=P0[:, d, :], rhs=cst(C_ID), start=True, stop=True), r=[KJ('P0'), 'consts'], w=tbk)
                    sch.op('act', lambda tv=tv: nc.scalar.copy(out=PT0[:], in_=tv), r=tbk, w=[KJ('PT0')])
                    for d in range(2):
                        sch.op('dve', lambda d=d: nc.vector.tensor_tensor(out=X0[:, d, :], in0=P0[:, d, :], in1=cst(C_ID), op=ALU.add), r=[KJ('P0'), 'consts'], w=[KJ('X0')])
                    if SS == 2 and p == int(os.environ.get('SSP', '0')):
                        dump('oacc', oacc[:], 'oacc')
                        return True
                    cur = 0
                    for lvl in range(1, 1 + int(os.environ.get('LVLS', '5'))):
                        nx = 1 - cur
                        Pc, PTc, Xc = Pm[rj][cur], PTm[rj][cur], Xm[rj][cur]
                        Pn, PTn, Xn = Pm[rj][nx], PTm[rj][nx], Xm[rj][nx]
                        kc = lambda nm, c_=cur: (nm + str(c_), rj)
                        kn = lambda nm, c_=nx: (nm + str(c_), rj)
                        _b, _k = psbank()
                        pt2, pt2k = _b[:, 0:256], [_k]
                        for d in range(2):
                            sch.op('pe', lambda d=d, pt2=pt2, Pc=Pc, PTc=PTc: nc.tensor.matmul(pt2[:, d * 128:(d + 1) * 128], lhsT=Pc[:, d, :], rhs=PTc[:, d, :], start=True, stop=True), r=[kc('P'), kc('PT')], w=pt2k)
                        sch.op('act', lambda pt2=pt2, PTn=PTn: nc.scalar.copy(out=PTn[:, :, :].rearrange("p a b -> p (a b)"), in_=pt2), r=pt2k, w=[kn('PT')])
                        if lvl < 5:
                            _b, _k = psbank()
                            p2, p2k = _b[:, 0:256], [_k]
                            for d in range(2):
                                sch.op('pe', lambda d=d, p2=p2, Pc=Pc, PTc=PTc: nc.tensor.matmul(p2[:, d * 128:(d + 1) * 128], lhsT=PTc[:, d, :], rhs=Pc[:, d, :], start=True, stop=True), r=[kc('P'), kc('PT')], w=p2k)
                            sch.op('act', lambda p2=p2, Pn=Pn: nc.scalar.copy(out=Pn[:, :, :].rearrange("p a b -> p (a b)"), in_=p2), r=p2k, w=[kn('P')])
                        _b, _k = psbank()
                        xp, xpk = _b[:, 0:256], [_k]
                        for d in range(2):
                            sch.op('pe', lambda d=d, xp=xp, PTn=PTn, Xc=Xc: nc.tensor.matmul(xp[:, d * 128:(d + 1) * 128], lhsT=PTn[:, d, :], rhs=Xc[:, d, :], start=True, stop=True), r=[kn('PT'), kc('X')], w=xpk)
                        sch.op('dve', lambda xp=xp, Xn=Xn, Xc=Xc: nc.vector.tensor_tensor(out=Xn[:, :, :].rearrange("p a b -> p (a b)"), in0=xp, in1=Xc[:, :, :].rearrange("p a b -> p (a b)"), op=ALU.add), r=xpk + [kc('X')], w=[kn('X')])
                        cur = nx
                    Xf32 = Xm[rj][cur]
                    sch.op('act', lambda Xf32=Xf32: nc.scalar.copy(out=Xbf[ri][:], in_=Xf32[:]), r=[('X' + str(cur), rj)], w=[('Xbf', ri)])
                    Xf = Xbf[ri]
                    Xk = ('Xbf', ri)
                    for d in range(2):
                        sch.op('dve', lambda d=d: nc.vector.tensor_tensor(out=kgT[ri][:, d, :], in0=kT[:, tsl[d]], in1=eGb[ri][:, d, :], op=ALU.mult), r=['kT', K('eGb')], w=[K('kgT')])
                        sch.op('dve', lambda d=d: nc.vector.scalar_tensor_tensor(out=qdT[ri][:, d, :], in0=qT[:, tsl[d]], scalar=SCALE, in1=eGb[ri][:, d, :], op0=ALU.mult, op1=ALU.mult), r=['qT', K('eGb')], w=[K('qdT')])
                        sch.op('dve', lambda d=d: nc.vector.tensor_scalar(out=ktl[ri][:, d, :], in0=ktok[:, tiles[d], :], scalar1=kts[:, tiles[d], cols[d]:cols[d] + 1], scalar2=None, op0=ALU.mult), r=['ktok', 'kts'], w=[K('ktl')])
                    if h == 0 and si == 0 and p == 0:
                        dump('X', Xf[:], Xk)
                        dump('qkT', qkT[ri][:], K('qkT'))
                        dump('eGb', eGb[ri][:], K('eGb'))
                        dump('P0', Pm[rj][0][:], KJ('P0'))
                    if SS == 3:
                        dump('oacc', oacc[:], 'oacc')
                        return True
                    ops_ = [(L['psb_'][OPSB][:, d * 128:(d + 1) * 128], ('ps', OPSB)) for d in range(2)]
                    for jj in range(0 if ((p >= 1 and os.environ.get('SKIPC2') == '1') or os.environ.get('SKIPC2') == '2') else 2):
                        for d in range(2):
                            j = jj if d == 0 else 1 - jj
                            cs = slice(64 * j, 64 * j + 64)
                            a_, ak = psq()
                            sch.op('pe', lambda d=d, a_=a_: nc.tensor.matmul(a_, lhsT=kgT[ri][:, d, :], rhs=Sb[d][:], start=True, stop=True), r=[K('kgT'), ('Sb', d)], w=[ak])
                            sch.op('dve', lambda d=d, a_=a_: nc.vector.tensor_tensor(out=rr[d][:], in0=vtok[:, tiles[d], :], in1=a_, op=ALU.subtract), r=['vtok', ak], w=[('rr', d)])
                            v_, vk = psq()
                            sch.op('pe', lambda d=d, v_=v_: nc.tensor.matmul(v_, lhsT=Xf[:, d, :], rhs=rr[d][:], start=True, stop=True), r=[Xk, ('rr', d)], w=[vk])
                            sch.op('act', lambda d=d, j=j, cs=cs, v_=v_: nc.scalar.activation(out=vnew[d][j][cs, :], in_=v_[cs, :], func=AF.Copy, scale=beta[cs, tiles[d], cols[d]:cols[d] + 1]),
                                   r=[vk, 'beta'], w=[('vnew', d, j)])
                            o_, ok_ = ops_[d]
                            sch.op('pe', lambda d=d, j=j, cs=cs, o_=o_: (nc.tensor.matmul(o_[:, cs], lhsT=Sb[d][:], rhs=qdT[ri][:, d, cs], start=True, stop=False),
                                                                 nc.tensor.matmul(o_[:, cs], lhsT=vnew[d][j][:], rhs=qkT[ri][:, d, cs], start=False, stop=True))[1],
                                   r=[('Sb', d), K('qdT'), ('vnew', d, j), K('qkT')], w=[ok_])
                            ds_, dsk = psq()
                            sch.op('pe', lambda d=d, j=j, ds_=ds_: nc.tensor.matmul(ds_, lhsT=ktl[ri][:, d, :], rhs=vnew[d][j][:], start=True, stop=True), r=[K('ktl'), ('vnew', d, j)], w=[dsk])
                            gl = 64 * j + 63 if d == 0 else 64 * j
                            sch.op('dve', lambda d=d, ds_=ds_, gl=gl: nc.vector.scalar_tensor_tensor(out=S32[d][:], in0=S32[d][:], scalar=eGb[ri][:, d, gl:gl + 1], in1=ds_, op0=ALU.mult, op1=ALU.add),
                                   r=[('S32', d), K('eGb'), dsk], w=[('S32', d)])
                            sch.op('act', lambda d=d: nc.scalar.copy(out=Sb[d][:], in_=S32[d][:]), r=[('S32', d)], w=[('Sb', d)])
                    for d in range(2):
                        if (p >= 1 and os.environ.get('SKIPC2') == '1') or os.environ.get('SKIPC2') == '2':
                            break
                        o_, ok_ = ops_[d]
                        sch.op('dve', lambda d=d, o_=o_: nc.vector.tensor_tensor(out=oacc[:, tsl[d]], in0=o_, in1=oacc[:, tsl[d]], op=ALU.add), r=[ok_, 'oacc'], w=['oacc'])
                    if BARR >= 1:
                        sch.barrier()
                    if SS == 4 or (SS == 6 and p + 1 >= int(os.environ.get('PMAX', '2'))):
                        dump('oacc', oacc[:], 'oacc')
                        return True
                if SS == 5:
                    dump('oacc', oacc[:], 'oacc')
                    return True
                if si > 0:
                    for d in range(2):
                        sch.dma('sp', [(sgo[si - 1, d, h], S32[d][:])], r=[('S32', d)], w=[('sgo', si, d, h)], slot=('sst', d))
            if h == 0:
                dump('oacc', oacc[:], 'oacc')
                if stop_after == 'g2':
                    return True
            finalize_head(nc, sch, L, oacc, zs, onv, og, h, sqh, rsh, sl)
        return False


def finalize_head(nc, sch, L, oacc, zs, onv, og, h, sqh, rsh, sl):
    psbank, onesb = L['psbank'], L['onesb']
    for c in range(NCH):
        i2 = c % 2
        csl = slice(c * 512, (c + 1) * 512)
        sch.op('act', lambda i2=i2, csl=csl: nc.scalar.activation(out=sqh[:, :], in_=oacc[:, csl], func=AF.Square), r=['oacc'], w=['sqh'])
        ss, sskey = psbank()
        sch.op('pe', lambda ss=ss, i2=i2: nc.tensor.matmul(ss[:, :], lhsT=onesb[:], rhs=sqh[:, :], start=True, stop=True), r=['sqh', 'onesb'], w=[sskey])
        sch.op('act', lambda ss=ss, i2=i2: nc.scalar.activation(out=rsh[:, :], in_=ss[:, :], func=AF.Sqrt, bias=EPS, scale=1.0 / 128), r=[sskey], w=['rsh'])
        sch.op('dve', lambda i2=i2: nc.vector.reciprocal(out=rsh[:, :], in_=rsh[:, :]), r=['rsh'], w=['rsh'])
        sch.op('dve', lambda i2=i2, csl=csl: nc.vector.tensor_tensor(out=sl[:, :], in0=oacc[:, csl], in1=rsh[:, :], op=ALU.mult), r=['oacc', 'rsh'], w=['sl'])
        sch.op('dve', lambda i2=i2, csl=csl: nc.vector.scalar_tensor_tensor(out=og[:, h, csl], in0=sl[:, :], scalar=onv[:, 0:1], in1=zs[:, csl], op0=ALU.mult, op1=ALU.mult),
               r=['sl', 'onv', 'zs'], w=[('og', c)])


def hgrn_mixer(nc, sch, L):
    hA, og, consts, onesb, identb = L['hA'], L['og'], L['consts'], L['onesb'], L['identb']
    psbank, next_slab, mm_group, cst, dump, psb_ = L['psbank'], L['next_slab'], L['mm_group'], L['cst'], L['dump'], L['psb_']
    hgrn_w_in, lblT, hgrn_on, sh0, sho = L['hgrn_w_in'], L['lblT'], L['hgrn_on'], L['sh0'], L['sho']
    stop_after = L['stop_after']
    with contextlib.ExitStack() as es:
        def sb(name, shape, dt):
            return es.enter_context(nc.sbuf_tensor(_un(name), list(shape), dt))
        lbl = sb("lbl", [128, 2, 16], F32)
        lbv = sb("lbv", [128, 16], F32)
        lbt = sb("lbt", [128, 16], F32)
        onv = sb("onv", [128, 1], F32)
        dl = sb("dl", [128, 256], F32)
        lbB = sb("lbB", [128, 256], F32)
        omB = sb("omB", [128, 256], F32)
        sg = [sb(f"sg{i}", [128, 256], F32) for i in range(2)]
        logf = sb("logf", [128, NTL, 2, 128], F32)
        ktok = sb("ktok", [128, NTL, 2, 128], BF16)
        vtok = sb("vtok", [128, NTL, 128], BF16)
        qsT = sb("qsT", [128, T], BF16)
        zs = sb("zs", [128, T], BF16)
        oacc = sb("oacc", [128, T], F32)
        sqh = sb("sqh", [128, 512], BF16)
        rsh = sb("rsh", [128, 512], F32)
        sl = sb("sl", [128, 512], F32)
        Vm = [sb(f"Vm{i}", [128, 8, 128], BF16) for i in range(2)]
        E2 = [[sb(f"E2_{i}_{d}", [128, 256], F32) for d in range(2)] for i in range(2)]
        enb = [[sb(f"enb_{i}_{d}", [128, 128], F32) for d in range(2)] for i in range(2)]
        fl = [[sb(f"fl_{i}_{d}", [128, 8], F32) for d in range(2)] for i in range(2)]
        qin = [[sb(f"qin_{i}_{d}", [128, 128], BF16) for d in range(2)] for i in range(2)]
        kout = [[sb(f"kout_{i}_{d}", [128, 128], BF16) for d in range(2)] for i in range(2)]
        ktl = [[sb(f"ktl_{i}_{d}", [128, 128], BF16) for d in range(2)] for i in range(2)]
        att = [[sb(f"att_{i}_{d}", [128, 128], BF16) for d in range(2)] for i in range(2)]
        S32 = [sb(f"S32_{d}", [128, 128], F32) for d in range(2)]
        Sb = [sb(f"Sb_{d}", [128, 128], BF16) for d in range(2)]

        sch.dma('sp', [(lbl[:], lblT)], w=['lbl'], slot='gs0')
        sch.dma('sp', [(onv[:], hgrn_on)], w=['onv'], slot='gs1')
        sch.op('act', lambda: nc.scalar.activation(out=lbl[:], in_=lbl[:], func=AF.Exp), r=['lbl'], w=['lbl'])
        sch.op('dve', lambda: nc.vector.tensor_tensor(out=lbt[:], in0=lbl[:, 0, :], in1=lbl[:, 1, :], op=ALU.add), r=['lbl'], w=['lbt'])
        sch.op('dve', lambda: nc.vector.reciprocal(out=lbt[:], in_=lbt[:]), r=['lbt'], w=['lbt'])
        sch.op('dve', lambda: nc.vector.tensor_tensor(out=lbv[:], in0=lbl[:, 1, :], in1=lbt[:], op=ALU.mult), r=['lbl', 'lbt'], w=['lbv'])

        for h in range(8):
            slab, skey = next_slab()
            sv = slab[:, 0:5120].rearrange("p (k c n) -> p k c n", k=8, c=5)
            offs = [h * 128, 4096 + h * 128, 1024 + h * 128, 2048 + h * 128, 3072 + h * 128]
            sch.dma('pool', [(sv[:, :, ci, :], hgrn_w_in[:, off:off + 128].rearrange("(k p) n -> p k n", p=128)) for ci, off in enumerate(offs)],
                    w=[skey], slot=skey)
            for d in range(2):
                sch.op('dve', lambda d=d: nc.vector.tensor_scalar(out=dl[:, d * 128:(d + 1) * 128], in0=cst(C_ID), scalar1=lbv[:, d * 8 + h:d * 8 + h + 1], scalar2=None, op0=ALU.mult),
                       r=['consts', 'lbv'], w=['dl'])
            pb, pbk = psbank()
            sch.op('pe', lambda pb=pb: nc.tensor.matmul(pb[:, 0:256], lhsT=cst(C_ONES), rhs=dl[:], start=True, stop=True), r=['consts', 'dl'], w=[pbk])
            sch.op('act', lambda pb=pb: nc.scalar.copy(out=lbB[:], in_=pb[:, 0:256]), r=[pbk], w=['lbB'])
            sch.op('dve', lambda: nc.vector.tensor_scalar(out=omB[:], in0=lbB[:], scalar1=-1.0, scalar2=1.0, op0=ALU.mult, op1=ALU.add), r=['lbB'], w=['omB'])
            for ct, (dst, dk) in enumerate(((qsT, 'qsT'), (zs, 'zs'))):
                for c in range(NCH):
                    ps, pkey = psbank()
                    sch.op('pe', mm_group(ps[:, :], [(sv[:, k, ct, :], hA[:, k, c * 512:(c + 1) * 512]) for k in range(8)]), r=[skey, ('hA', c)], w=[pkey])
                    sch.op('act', lambda ps=ps, dst=dst, c=c: nc.scalar.activation(out=dst[:, c * 512:(c + 1) * 512], in_=ps[:, :], func=AF.Silu), r=[pkey], w=[dk])
            for i in range(NTL):
                ps, pkey = psbank()
                sch.op('pe', mm_group(ps[:, 0:384], [(hA[:, k, i * 128:(i + 1) * 128], sv[:, k, 2:5, :].rearrange("p c n -> p (c n)")) for k in range(8)]),
                       r=[skey, ('hA', i // 4)], w=[pkey])
                sgi = sg[i % 2]
                sgk = ('sg', i % 2)
                sch.op('act', lambda ps=ps, sgi=sgi: nc.scalar.activation(out=sgi[:], in_=ps[:, 0:256], func=AF.Sigmoid), r=[pkey], w=[sgk])
                sch.op('act', lambda ps=ps, i=i: nc.scalar.copy(out=vtok[:, i, :], in_=ps[:, 256:384]), r=[pkey], w=['vtok'])
                sch.op('dve', lambda sgi=sgi: nc.vector.tensor_tensor(out=sgi[:], in0=sgi[:], in1=omB[:], op=ALU.mult), r=[sgk, 'omB'], w=[sgk])
                sch.op('dve', lambda sgi=sgi: nc.vector.tensor_tensor(out=sgi[:], in0=sgi[:], in1=lbB[:], op=ALU.add), r=[sgk, 'lbB'], w=[sgk])
                sch.op('act', lambda sgi=sgi, i=i: nc.scalar.activation(out=logf[:, i, :, :].rearrange("p a b -> p (a b)"), in_=sgi[:], func=AF.Ln), r=[sgk], w=['logf'])
                sch.op('dve', lambda sgi=sgi, i=i: nc.vector.tensor_scalar(out=ktok[:, i, :, :].rearrange("p a b -> p (a b)"), in0=sgi[:], scalar1=-1.0, scalar2=1.0, op0=ALU.mult, op1=ALU.add), r=[sgk], w=['ktok'])
            if h == 0:
                dump('qsT', qsT[:], 'qsT')
                dump('logf', logf[:], 'logf')
                dump('ktokh', ktok[:], 'ktok')
                dump('vtokh', vtok[:], 'vtok')
                if stop_after == 'h1':
                    return True
            sch.op('dve', lambda: nc.vector.memset(oacc[:], 0.0), w=['oacc'])

            npair = 0
            for si, (t0, n) in enumerate(SEQS):
                if si == 0:
                    for d in range(2):
                        sch.dma('sp', [(S32[d][:], sh0[d, h])], w=[('S32', d)], slot=('S32', d))
                for d in range(2):
                    if si != 0:
                        sch.op('dve', lambda d=d: nc.vector.memset(S32[d][:], 0.0), w=[('S32', d)])
                    sch.op('act', lambda d=d: nc.scalar.copy(out=Sb[d][:], in_=S32[d][:]), r=[('S32', d)], w=[('Sb', d)])
                for p in range(n):
                    ri = npair % 2
                    npair += 1
                    tiles = [t0 + p, t0 + n - 1 - p]
                    tsl = [slice(i * 128, (i + 1) * 128) for i in tiles]
                    vms = []
                    for d in range(2):
                        if d == 1 and tiles[1] == tiles[0]:
                            vms.append(vms[0])
                            continue
                        vi = (2 * npair + d) % 2
                        vmk = ('Vm', vi)
                        for nn in range(8):
                            if nn % 2 == 0:
                                sch.op('dve', lambda vi=vi, nn=nn, d=d: nc.vector.tensor_scalar(out=Vm[vi][:, nn, :], in0=vtok[:, tiles[d], :], scalar1=cst(C_CM)[:, nn:nn + 1], scalar2=None, op0=ALU.mult),
                                       r=['vtok', 'consts'], w=[vmk])
                            else:
                                sch.op('act', lambda vi=vi, nn=nn, d=d: nc.scalar.activation(out=Vm[vi][:, nn, :], in_=vtok[:, tiles[d], :], func=AF.Copy, scale=cst(C_CM)[:, nn:nn + 1]),
                                       r=['vtok', 'consts'], w=[vmk])
                        vms.append((Vm[vi], vmk))
                    ops_ = [(psb_[OPSB], ('ps', OPSB)), (psb_[OPSB + 1], ('ps', OPSB + 1))]
                    for d in range(2):
                        i = tiles[d]
                        K = lambda name, d=d: (name, ri, d)
                        Lg = logf[:, i, d, :]
                        cb, cbk = psbank()
                        sch.op('pe', lambda cb=cb, Lg=Lg, d=d: nc.tensor.matmul(cb[:, 0:128], lhsT=Lg, rhs=cst(C_MF16 if d == 0 else C_MB16), start=True, stop=True), r=['logf', 'consts'], w=[cbk])
                        sch.op('pe', lambda cb=cb, Lg=Lg, d=d: nc.tensor.matmul(cb[:, 128:256], lhsT=cst(C_SB16 if d == 0 else C_SF16), rhs=Lg, start=True, stop=True), r=['logf', 'consts'], w=[cbk])
                        sch.op('pe', lambda cb=cb, Lg=Lg: nc.tensor.matmul(cb[:, 256:264], lhsT=Lg, rhs=cst(C_CM)[:, 0:8], start=True, stop=True), r=['logf', 'consts'], w=[cbk])
                        sch.op('act', lambda cb=cb, d=d: nc.scalar.activation(out=E2[ri][d][:], in_=cb[:, 0:256], func=AF.Exp), r=[cbk], w=[K('E2')])
                        sch.op('act', lambda cb=cb, d=d: nc.scalar.activation(out=enb[ri][d][:], in_=cb[:, 0:128], func=AF.Exp, scale=-1.0), r=[cbk], w=[K('enb')])
                        sch.op('act', lambda cb=cb, d=d: nc.scalar.activation(out=fl[ri][d][:], in_=cb[:, 256:264], func=AF.Exp), r=[cbk], w=[K('fl')])
                        kt, ktk = psbank()
                        sch.op('pe', lambda kt=kt, i=i, d=d: nc.tensor.matmul(kt[:, 0:128], lhsT=ktok[:, i, d, :], rhs=identb[:], start=True, stop=True), r=['ktok', 'identb'], w=[ktk])
                        sch.op('dve', lambda d=d: nc.vector.tensor_tensor(out=qin[ri][d][:], in0=qsT[:, tsl[d]], in1=E2[ri][d][:, 0:128], op=ALU.mult), r=['qsT', K('E2')], w=[K('qin')])
                        sch.op('dve', lambda kt=kt, d=d: nc.vector.tensor_tensor(out=kout[ri][d][:], in0=kt[:, 0:128], in1=enb[ri][d][:], op=ALU.mult), r=[ktk, K('enb')], w=[K('kout')])
                        sch.op('dve', lambda i=i, d=d: nc.vector.tensor_tensor(out=ktl[ri][d][:], in0=ktok[:, i, d, :], in1=E2[ri][d][:, 128:256], op=ALU.mult), r=['ktok', K('E2')], w=[K('ktl')])
                        ap_, apk = psbank()
                        sch.op('pe', lambda ap_=ap_, d=d: nc.tensor.matmul(ap_[:, 0:128], lhsT=kout[ri][d][:], rhs=qin[ri][d][:], start=True, stop=True), r=[K('kout'), K('qin')], w=[apk])
                        sch.op('dve', lambda ap_=ap_, d=d: nc.vector.tensor_tensor(out=att[ri][d][:], in0=ap_[:, 0:128], in1=cst(C_MF16 if d == 0 else C_MB16), op=ALU.mult), r=[apk, 'consts'], w=[K('att')])
                        o_, ok_ = ops_[d]
                        sch.op('pe', lambda o_=o_, i=i, d=d: nc.tensor.matmul(o_[:, 0:128], lhsT=vtok[:, i, :], rhs=att[ri][d][:], start=True, stop=False, skip_group_check=True), r=['vtok', K('att')], w=[ok_])
                    for jj in range(8):
                        for d in range(2):
                            K = lambda name, d=d: (name, ri, d)
                            nn = jj if d == 0 else 7 - jj
                            cn = slice(16 * nn, 16 * nn + 16)
                            o_, ok_ = ops_[d]
                            sch.op('pe', lambda o_=o_, d=d, cn=cn, jj=jj: nc.tensor.matmul(o_[:, cn], lhsT=Sb[d][:], rhs=qin[ri][d][:, cn], start=False, stop=(jj == 7), skip_group_check=True),
                                   r=[('Sb', d), K('qin')], w=[ok_])
                            ds_, dsk = psbank()
                            vmt, vmk = vms[d]
                            sch.op('pe', lambda ds_=ds_, d=d, nn=nn, vmt=vmt: nc.tensor.matmul(ds_[:, 0:128], lhsT=ktl[ri][d][:], rhs=vmt[:, nn, :], start=True, stop=True), r=[K('ktl'), vmk], w=[dsk])
                            sch.op('dve', lambda d=d, ds_=ds_, nn=nn: nc.vector.scalar_tensor_tensor(out=S32[d][:], in0=S32[d][:], scalar=fl[ri][d][:, nn:nn + 1], in1=ds_[:, 0:128], op0=ALU.mult, op1=ALU.add),
                                   r=[('S32', d), K('fl'), dsk], w=[('S32', d)])
                            sch.op('act', lambda d=d: nc.scalar.copy(out=Sb[d][:], in_=S32[d][:]), r=[('S32', d)], w=[('Sb', d)])
                    for d in range(2):
                        o_, ok_ = ops_[d]
                        sch.op('dve', lambda d=d, o_=o_: nc.vector.tensor_tensor(out=oacc[:, tsl[d]], in0=o_[:, 0:128], in1=oacc[:, tsl[d]], op=ALU.add), r=[ok_, 'oacc'], w=['oacc'])
                if si > 0:
                    for d in range(2):
                        sch.dma('sp', [(sho[si - 1, d, h], S32[d][:])], r=[('S32', d)], w=[('sho', si, d, h)], slot=('sst', d))
            if h == 0:
                dump('oacc', oacc[:], 'oacc')
                if stop_after == 'h2':
                    return True
            finalize_head(nc, sch, L, oacc, zs, onv, og, h, sqh, rsh, sl)
        return False


_NC_CACHE = {}


def host_inputs(core, inp, consts):
    f = np.float32
    xT = np.concatenate([inp['x_sample'][core].T, inp['x_prompt'][2 * core].T, inp['x_prompt'][2 * core + 1].T], axis=1)
    cT = np.stack([inp['c'][core], inp['c_ctx']], axis=1)
    m = {
        'xT': np.ascontiguousarray(xT, f),
        'cT': np.ascontiguousarray(cT, f),
        'consts': consts,
        'w_ada': inp['w_ada'],
        'b_adaT': np.ascontiguousarray(inp['b_ada'].reshape(2, 48, 128).transpose(2, 0, 1), f),
        'norm_gT': np.ascontiguousarray(inp['norm_g'].reshape(2, 4, 8, 128).transpose(3, 0, 1, 2), f),
        'gdn_w_in': inp['gdn_w_in'][0],
        'conv_wT': np.ascontiguousarray(inp['gdn_conv_w'][0].reshape(5, 24, 128).transpose(2, 1, 0), f),
        'alog10': np.ascontiguousarray(np.broadcast_to(inp['gdn_a_log'][0].reshape(1, 1, 16), (128, 10, 16)).reshape(128, 160), f),
        'dtb10': np.ascontiguousarray(np.broadcast_to(inp['gdn_dt_bias'][0].reshape(1, 1, 16), (128, 10, 16)).reshape(128, 160), f),
        'gdn_on': np.ascontiguousarray(inp['gdn_onorm_g'][0].reshape(128, 1), f),
        'gdn_w_out': inp['gdn_w_out'][0],
        'hgrn_w_in': inp['hgrn_w_in'][0],
        'lblT': np.ascontiguousarray(inp['hgrn_lb_logits'].reshape(2, 2, 8, 128).transpose(3, 0, 1, 2).reshape(128, 2, 16), f),
        'hgrn_on': np.ascontiguousarray(inp['hgrn_onorm_g'][0].reshape(128, 1), f),
        'hgrn_w_out': inp['hgrn_w_out'][0],
        'mlp_w1': inp['mlp_w1'],
        'mlp_w2': inp['mlp_w2'],
        'sg0': np.ascontiguousarray(inp['state_gdn'][core, 0], f),
        'sh0': np.ascontiguousarray(inp['state_hgrn'][core, 0], f),
    }
    return m


def kernel(**inputs):
    inp = {k: np.asarray(v) for k, v in inputs.items()}
    if 'nc' not in _NC_CACHE:
        _NC_CACHE['nc'] = build()
    nc = _NC_CACHE['nc']
    consts = make_consts()
    in_maps = [host_inputs(core, inp, consts) for core in range(8)]
    res = run_bass_kernel_spmd(nc, in_maps, core_ids=list(range(8)))
    y_prompt = np.empty((16, 256, 1024), np.float32)
    y_sample = np.empty((8, 2048, 1024), np.float32)
    ng = np.empty((16, 1, 2, 8, 128, 128), np.float32)
    nh = np.empty((16, 1, 2, 8, 128, 128), np.float32)
    for core in range(8):
        r = res.results[core]
        yT = r['yT']
        y_sample[core] = yT[:, 0:2048].T
        y_prompt[2 * core] = yT[:, 2048:2304].T
        y_prompt[2 * core + 1] = yT[:, 2304:2560].T
        ng[2 * core:2 * core + 2, 0] = r['sgo']
        nh[2 * core:2 * core + 2, 0] = r['sho']
    return (y_prompt, y_sample, ng, nh)
```
